# Optimizing a Trainium2 kernel written in Bass

```python
import math
import jax
import jax.numpy as jnp
from jax import lax
import numpy as np

D_MODEL = 2048
BATCH = 4
SEQ = 4096
DEPTH = 4

CTX_LEN = 256
GRID_W = 64
N_MOD = 6
NORM_EPS = 1e-6
NEG_INF = -1e30
ATTN_HEAD_DIM = 128
ATTN_Q_HEADS = (D_MODEL // 2) // ATTN_HEAD_DIM
ATTN_KV_HEADS = max(1, ATTN_Q_HEADS // 4)
ATTN_GROUP = ATTN_Q_HEADS // ATTN_KV_HEADS
WINDOW = 128
ATTN_BLOCK = 128
ROPE_THETA = 10000.0
GDN_HEAD_DIM = 128
GDN_HEADS = (D_MODEL // 2) // GDN_HEAD_DIM
GDN_CONV = 5
DELTA_CHUNK = 64
N_DIR = 2
FOURIER_GROUPS = 8
FOURIER_GROUP_DIM = D_MODEL // FOURIER_GROUPS
FFN_HIDDEN = -(-(8 * D_MODEL) // (3 * 256)) * 256
ATTN_W = ATTN_Q_HEADS * ATTN_HEAD_DIM
KV_W = ATTN_KV_HEADS * ATTN_HEAD_DIM
GDN_W = GDN_HEADS * GDN_HEAD_DIM
IN_SPLITS = (ATTN_W, KV_W, KV_W, 3 * GDN_W, GDN_W, N_DIR * GDN_HEADS, N_DIR * GDN_HEADS)
IN_WIDTH = sum(IN_SPLITS)
IN_OFFSETS = tuple(int(o) for o in np.cumsum(IN_SPLITS)[:-1])
MIX_WIDTH = ATTN_W + GDN_W

kernel_name = 'hybrid_swa_gdn_fourier_dit_trunk'


def rms_norm(x, gain):
    xf = x.astype(jnp.float32)
    y = xf * lax.rsqrt(jnp.mean(xf * xf, axis=-1, keepdims=True) + NORM_EPS)
    return (y * gain.astype(jnp.float32)).astype(x.dtype)


def modulate(h, shift, scale):
    return h * (1 + scale) + shift


def split_heads(t, n_heads):
    return t.reshape(t.shape[0], t.shape[1], n_heads, -1)


def axial_rope(n_rows):
    quarter = ATTN_HEAD_DIM // 4
    inv_freq = ROPE_THETA ** (-jnp.arange(quarter, dtype=jnp.float32) / quarter)
    rows = jnp.repeat(jnp.arange(n_rows, dtype=jnp.float32), GRID_W)
    cols = jnp.tile(jnp.arange(GRID_W, dtype=jnp.float32), n_rows)
    ang = jnp.concatenate([rows[:, None] * inv_freq, cols[:, None] * inv_freq], axis=-1)
    return jnp.cos(ang), jnp.sin(ang)


def apply_rope(t, cos, sin):
    tf = t.astype(jnp.float32)
    t1, t2 = jnp.split(tf, 2, axis=-1)
    c = cos[None, :, None, :]
    s = sin[None, :, None, :]
    return jnp.concatenate([t1 * c - t2 * s, t2 * c + t1 * s], axis=-1).astype(t.dtype)


def sink_softmax(parts, sink):
    lead = parts[0].shape[:-1]
    s = jnp.broadcast_to(sink.astype(jnp.float32).reshape(ATTN_KV_HEADS, ATTN_GROUP, 1, 1), lead + (1,))
    p = jax.nn.softmax(jnp.concatenate(list(parts) + [s], axis=-1), axis=-1)
    offsets = [int(o) for o in np.cumsum([t.shape[-1] for t in parts])]
    return jnp.split(p[..., :offsets[-1]], offsets[:-1], axis=-1)


def windowed_attention(q, k, v, k_ctx, v_ctx, sink):
    b, l, _, d = q.shape
    nb = l // ATTN_BLOCK
    scale = d ** -0.5
    qb = q.reshape(b, nb, ATTN_BLOCK, ATTN_KV_HEADS, ATTN_GROUP, d)

    def band(t):
        tp = jnp.pad(t, ((0, 0), (ATTN_BLOCK, ATTN_BLOCK), (0, 0), (0, 0)))
        tp = tp.reshape(b, nb + 2, ATTN_BLOCK, ATTN_KV_HEADS, d)
        return jnp.concatenate([tp[:, :-2], tp[:, 1:-1], tp[:, 2:]], axis=2)

    kb, vb = band(k), band(v)
    s_win = jnp.einsum('bnqhgd,bnkhd->bnhgqk', qb, kb, preferred_element_type=jnp.float32) * scale
    s_ctx = jnp.einsum('bnqhgd,bchd->bnhgqc', qb, k_ctx, preferred_element_type=jnp.float32) * scale
    blk = jnp.arange(nb)[:, None, None]
    qi = jnp.arange(ATTN_BLOCK)[None, :, None]
    kj = jnp.arange(3 * ATTN_BLOCK)[None, None, :]
    qpos = blk * ATTN_BLOCK + qi
    kpos = (blk - 1) * ATTN_BLOCK + kj
    valid = (jnp.abs(qpos - kpos) <= WINDOW) & (kpos >= 0) & (kpos < l)
    s_win = jnp.where(valid[None, :, None, None], s_win, NEG_INF)
    p_win, p_ctx = sink_softmax([s_win, s_ctx], sink)
    o = (jnp.einsum('bnhgqk,bnkhd->bnqhgd', p_win.astype(v.dtype), vb)
         + jnp.einsum('bnhgqc,bchd->bnqhgd', p_ctx.astype(v.dtype), v_ctx))
    return o.reshape(b, l, ATTN_Q_HEADS * d)


def context_attention(q, k, v, sink):
    b, lc, _, d = q.shape
    qg = q.reshape(b, lc, ATTN_KV_HEADS, ATTN_GROUP, d)
    s = jnp.einsum('bqhgd,bkhd->bhgqk', qg, k, preferred_element_type=jnp.float32) * (d ** -0.5)
    (p,) = sink_softmax([s], sink)
    o = jnp.einsum('bhgqk,bkhd->bqhgd', p.astype(v.dtype), v)
    return o.reshape(b, lc, ATTN_Q_HEADS * d)


def short_conv(t, w):
    ch = t.shape[-1]
    return lax.conv_general_dilated(
        t, w[:, None, :].astype(t.dtype), window_strides=(1,),
        padding=((GDN_CONV // 2, GDN_CONV // 2),),
        dimension_numbers=('NWC', 'WIO', 'NWC'), feature_group_count=ch)


def l2_normalize(t):
    return t * lax.rsqrt(jnp.sum(t * t, axis=-1, keepdims=True) + NORM_EPS)


def deltanet_inputs(qkv_raw, a_raw, b_raw, w_conv, a_log, dt_bias):
    b, l, _ = qkv_raw.shape
    qkv = jax.nn.silu(short_conv(qkv_raw, w_conv)).astype(jnp.float32)
    q, k, v = jnp.split(qkv, 3, axis=-1)
    q = l2_normalize(split_heads(q, GDN_HEADS)) * (GDN_HEAD_DIM ** -0.5)
    k = l2_normalize(split_heads(k, GDN_HEADS))
    v = split_heads(v, GDN_HEADS)
    a = a_raw.astype(jnp.float32).reshape(b, l, N_DIR, GDN_HEADS)
    beta = jax.nn.sigmoid(b_raw.astype(jnp.float32).reshape(b, l, N_DIR, GDN_HEADS))
    g = -jnp.exp(a_log.astype(jnp.float32)) * jax.nn.softplus(a + dt_bias.astype(jnp.float32))
    return q, k, v, g, beta


def gated_delta_chunked(q, k, v, g, beta, state):
    b, l, h, _ = q.shape
    dv = v.shape[-1]
    n = l // DELTA_CHUNK

    def chunks(t):
        return jnp.moveaxis(t.reshape(b, n, DELTA_CHUNK, h, *t.shape[3:]), 3, 1)

    qc, kc, vc = chunks(q), chunks(k), chunks(v)
    gc = jnp.cumsum(chunks(g), axis=-1)
    bc = chunks(beta)
    causal = jnp.tril(jnp.ones((DELTA_CHUNK, DELTA_CHUNK), dtype=bool))
    strict = jnp.tril(jnp.ones((DELTA_CHUNK, DELTA_CHUNK), dtype=bool), k=-1)
    diff = gc[..., :, None] - gc[..., None, :]
    decay = jnp.where(causal, jnp.exp(jnp.where(causal, diff, 0.0)), 0.0)
    kb = kc * bc[..., None]
    m = jnp.where(strict, jnp.einsum('bhncd,bhnsd->bhncs', kb, kc) * decay, 0.0)
    rhs = jnp.concatenate([vc * bc[..., None], kb * jnp.exp(gc)[..., None]], axis=-1)
    sol = lax.linalg.triangular_solve(jnp.eye(DELTA_CHUNK, dtype=jnp.float32) + m, rhs,
                                      left_side=True, lower=True, unit_diagonal=True)
    u, w = sol[..., :dv], sol[..., dv:]
    attn = jnp.einsum('bhncd,bhnsd->bhncs', qc, kc) * decay
    q_dec = qc * jnp.exp(gc)[..., None]
    g_last = gc[..., -1]
    k_dec = kc * jnp.exp(g_last[..., None] - gc)[..., None]

    def step(s, xs):
        u_i, w_i, attn_i, q_i, k_i, gl_i = xs
        v_new = u_i - jnp.einsum('bhcd,bhde->bhce', w_i, s)
        o_i = jnp.einsum('bhcd,bhde->bhce', q_i, s) + jnp.einsum('bhcs,bhse->bhce', attn_i, v_new)
        s = s * jnp.exp(gl_i)[..., None, None] + jnp.einsum('bhcd,bhce->bhde', k_i, v_new)
        return s, o_i

    xs = tuple(jnp.moveaxis(t, 2, 0) for t in (u, w, attn, q_dec, k_dec, g_last))
    s_final, o = lax.scan(step, state, xs)
    o = jnp.moveaxis(jnp.moveaxis(o, 0, 2), 1, 3).reshape(b, l, h, dv)
    return o, s_final


def seq_flip(t, reverse):
    return jnp.flip(t, axis=1) if reverse else t


def bidirectional_deltanet(lat, ctx):
    qx, kx, vx, gx, bx = lat
    qc, kc, vc, gc, bc = ctx
    b, _, h, dk = qx.shape
    s0 = jnp.zeros((b, h, dk, vx.shape[-1]), jnp.float32)
    outs_x, outs_c = [], []
    for d in range(N_DIR):
        rev = d == 1
        o_c, s_c = gated_delta_chunked(seq_flip(qc, rev), seq_flip(kc, rev), seq_flip(vc, rev),
                                       seq_flip(gc[:, :, d], rev), seq_flip(bc[:, :, d], rev), s0)
        o_x, _ = gated_delta_chunked(seq_flip(qx, rev), seq_flip(kx, rev), seq_flip(vx, rev),
                                     seq_flip(gx[:, :, d], rev), seq_flip(bx[:, :, d], rev), s_c)
        outs_x.append(seq_flip(o_x, rev))
        outs_c.append(seq_flip(o_c, rev))
    return outs_x[0] + outs_x[1], outs_c[0] + outs_c[1]


def gated_output(o, z, gain):
    b, l = o.shape[0], o.shape[1]
    y = rms_norm(o, gain) * jax.nn.silu(split_heads(z, GDN_HEADS).astype(jnp.float32))
    return y.reshape(b, l, GDN_W)


def merge_heads(att, gdn, w_out):
    return jnp.concatenate([att, gdn.astype(att.dtype)], axis=-1) @ w_out


def attention_deltanet_mixer(h_x, h_c, w_in, w_conv, sink, a_log, dt_bias, norm_gain, w_out,
                             cos, sin, with_ctx_out):
    px = jnp.split(h_x @ w_in, IN_OFFSETS, axis=-1)
    pc = jnp.split(h_c @ w_in, IN_OFFSETS, axis=-1)
    q_x = apply_rope(split_heads(px[0], ATTN_Q_HEADS), cos, sin)
    k_x = apply_rope(split_heads(px[1], ATTN_KV_HEADS), cos, sin)
    v_x = split_heads(px[2], ATTN_KV_HEADS)
    k_c = split_heads(pc[1], ATTN_KV_HEADS)
    v_c = split_heads(pc[2], ATTN_KV_HEADS)
    att_x = windowed_attention(q_x, k_x, v_x, k_c, v_c, sink)
    d_x = deltanet_inputs(px[3], px[5], px[6], w_conv, a_log, dt_bias)
    d_c = deltanet_inputs(pc[3], pc[5], pc[6], w_conv, a_log, dt_bias)
    gdn_x, gdn_c = bidirectional_deltanet(d_x, d_c)
    y_x = merge_heads(att_x, gated_output(gdn_x, px[4], norm_gain), w_out)
    if not with_ctx_out:
        return y_x, None
    att_c = context_attention(split_heads(pc[0], ATTN_Q_HEADS), k_c, v_c, sink)
    y_c = merge_heads(att_c, gated_output(gdn_c, pc[4], norm_gain), w_out)
    return y_x, y_c


def fourier_mix(h):
    b, l, d = h.shape
    hg = h.astype(jnp.float32).reshape(b, l, FOURIER_GROUPS, FOURIER_GROUP_DIM)
    y = jnp.fft.fft2(hg, axes=(1, 3), norm='ortho').real
    return y.reshape(b, l, d).astype(h.dtype)


def swiglu(h, w_gate, w_up, w_down):
    return (jax.nn.silu(h @ w_gate) * (h @ w_up)) @ w_down


def setup_inputs(seed: int = 0) -> dict:
    key = jax.random.key(seed)
    ks = jax.random.split(key, 24)
    f32 = jnp.float32
    n_even = (DEPTH + 1) // 2
    n_odd = DEPTH // 2

    def normal(i, shape, scale):
        return jax.random.normal(ks[i], shape, f32) * scale

    x = normal(0, (BATCH, SEQ, D_MODEL), 1.0)
    c = normal(1, (BATCH, D_MODEL), 1.0)
    ctx = normal(2, (BATCH, CTX_LEN, D_MODEL), 1.0)
    c_ctx = normal(3, (D_MODEL,), 1.0)
    w_ada = normal(4, (DEPTH, D_MODEL, N_MOD * D_MODEL), D_MODEL ** -0.5)
    b_ada = normal(5, (DEPTH, N_MOD * D_MODEL), 0.02)
    g_pre_mix = 1.0 + normal(6, (DEPTH, D_MODEL), 0.05)
    g_post_mix = 1.0 + normal(7, (DEPTH, D_MODEL), 0.05)
    g_pre_ffn = 1.0 + normal(8, (DEPTH, D_MODEL), 0.05)
    g_post_ffn = 1.0 + normal(9, (DEPTH, D_MODEL), 0.05)
    w_in = normal(10, (n_even, D_MODEL, IN_WIDTH), D_MODEL ** -0.5)
    w_conv = normal(11, (n_even, GDN_CONV, 3 * GDN_W), GDN_CONV ** -0.5)
    attn_sink = normal(12, (n_even, ATTN_Q_HEADS), 0.5)
    gdn_a_log = jnp.log(jax.random.uniform(ks[13], (n_even, N_DIR, GDN_HEADS), f32, 1.0, 16.0))
    dt = jnp.exp(jax.random.uniform(ks[14], (n_even, N_DIR, GDN_HEADS), f32,
                                    math.log(1e-3), math.log(1e-1)))
    gdn_dt_bias = dt + jnp.log(-jnp.expm1(-dt))
    gdn_norm = 1.0 + normal(15, (n_even, GDN_HEAD_DIM), 0.05)
    w_out_mix = normal(16, (n_even, MIX_WIDTH, D_MODEL), MIX_WIDTH ** -0.5)
    w_fourier = normal(17, (n_odd, D_MODEL, D_MODEL), D_MODEL ** -0.5)
    w_gate = normal(18, (DEPTH, D_MODEL, FFN_HIDDEN), D_MODEL ** -0.5)
    w_up = normal(19, (DEPTH, D_MODEL, FFN_HIDDEN), D_MODEL ** -0.5)
    w_down = normal(20, (DEPTH, FFN_HIDDEN, D_MODEL), FFN_HIDDEN ** -0.5)
    return {'x': x, 'c': c, 'ctx': ctx, 'c_ctx': c_ctx, 'w_ada': w_ada, 'b_ada': b_ada,
            'g_pre_mix': g_pre_mix, 'g_post_mix': g_post_mix, 'g_pre_ffn': g_pre_ffn,
            'g_post_ffn': g_post_ffn, 'w_in': w_in, 'w_conv': w_conv, 'attn_sink': attn_sink,
            'gdn_a_log': gdn_a_log, 'gdn_dt_bias': gdn_dt_bias, 'gdn_norm': gdn_norm,
            'w_out_mix': w_out_mix, 'w_fourier': w_fourier, 'w_gate': w_gate, 'w_up': w_up,
            'w_down': w_down}


def reference(x, c, ctx, c_ctx, w_ada, b_ada, g_pre_mix, g_post_mix, g_pre_ffn, g_post_ffn,
              w_in, w_conv, attn_sink, gdn_a_log, gdn_dt_bias, gdn_norm, w_out_mix, w_fourier,
              w_gate, w_up, w_down):
    seq_len = x.shape[1]
    n_rows = seq_len // GRID_W
    cos, sin = axial_rope(n_rows)
    silu_c = jax.nn.silu(c)
    silu_cc = jax.nn.silu(c_ctx)
    for layer in range(DEPTH):
        even = layer % 2 == 0
        ctx_needed = any(j % 2 == 0 for j in range(layer + 1, DEPTH))
        uses_ctx = even or ctx_needed
        mx = [m[:, None, :] for m in jnp.split(silu_c @ w_ada[layer] + b_ada[layer], N_MOD, axis=-1)]
        mc = jnp.split(silu_cc @ w_ada[layer] + b_ada[layer], N_MOD, axis=-1)
        h_x = modulate(rms_norm(x, g_pre_mix[layer]), mx[0], mx[1])
        h_c = modulate(rms_norm(ctx, g_pre_mix[layer]), mc[0], mc[1]) if uses_ctx else None
        if even:
            e = layer // 2
            y_x, y_c = attention_deltanet_mixer(h_x, h_c, w_in[e], w_conv[e], attn_sink[e],
                                                gdn_a_log[e], gdn_dt_bias[e], gdn_norm[e],
                                                w_out_mix[e], cos, sin, ctx_needed)
        else:
            o = layer // 2
            y_x = fourier_mix(h_x) @ w_fourier[o]
            y_c = fourier_mix(h_c) @ w_fourier[o] if ctx_needed else None
        x = x + mx[2] * rms_norm(y_x, g_post_mix[layer])
        f_x = swiglu(modulate(rms_norm(x, g_pre_ffn[layer]), mx[3], mx[4]),
                     w_gate[layer], w_up[layer], w_down[layer])
        x = x + mx[5] * rms_norm(f_x, g_post_ffn[layer])
        if ctx_needed:
            ctx = ctx + mc[2] * rms_norm(y_c, g_post_mix[layer])
            f_c = swiglu(modulate(rms_norm(ctx, g_pre_ffn[layer]), mc[3], mc[4]),
                         w_gate[layer], w_up[layer], w_down[layer])
            ctx = ctx + mc[5] * rms_norm(f_c, g_post_ffn[layer])
    return x
```

```python
import math
import numpy as np
import concourse.bass as bass
import concourse.mybir as mybir
from concourse.bass_utils import run_bass_kernel_spmd

F32 = mybir.dt.float32
BF16 = mybir.dt.bfloat16
AF = mybir.ActivationFunctionType
ALU = mybir.AluOpType
AX = mybir.AxisListType

PE, ACT, DVE, POOL, SP = "pe", "act", "dve", "pool", "sp"
COMPUTE = (PE, ACT, DVE, POOL)


class Res:
    __slots__ = ("name", "t", "last_w", "readers", "dsem", "dcount", "excl")

    def __init__(self, name, t):
        self.name = name
        self.t = t
        self.last_w = None
        self.readers = []
        self.dsem = None
        self.dcount = 0
        self.excl = False

    def __getitem__(self, key):
        return self.t[key]


class View:
    def __init__(self, base, ap):
        self.base = base
        self.ap = ap

    def __getitem__(self, key):
        return self.ap[key]


class Op:
    __slots__ = ("eng", "fn", "deps", "signal", "cnt", "is_dma", "dres", "dtarget", "ndma")

    def __init__(self, eng, fn):
        self.eng = eng
        self.fn = fn
        self.deps = []
        self.signal = False
        self.cnt = None
        self.is_dma = False
        self.dres = None
        self.dtarget = 0
        self.ndma = 0


class Sched:
    def __init__(self, nc):
        self.nc = nc
        self.ops = []
        self.res = []
        self.esem = {}

    def sb(self, name, shape, dtype):
        self.uid = getattr(self, "uid", 0) + 1
        name = "%s_u%d" % (name, self.uid)
        t = self.nc.alloc_sbuf_tensor(name, list(shape), dtype)
        r = Res(name, t)
        self.res.append(r)
        return r

    def ps(self, name, shape, dtype=F32):
        t = self.nc.alloc_psum_tensor(name, list(shape), dtype)
        r = Res(name, t)
        r.excl = True
        self.res.append(r)
        return r

    def sub(self, name, ap):
        r = Res(name, ap)
        self.res.append(r)
        return r

    def _add_deps(self, op, reads, writes):
        reads = [getattr(r, "base", r) for r in reads]
        writes = [getattr(r, "base", r) for r in writes]
        writes = writes + [r for r in reads if r.excl and r not in writes]
        reads = [r for r in reads if not r.excl]
        deps = []
        for r in reads:
            if r.last_w is not None:
                deps.append((r.last_w, "raw", r))
        for r in writes:
            if r.last_w is not None:
                deps.append((r.last_w, "waw", r))
            for rd in r.readers:
                deps.append((rd, "war", r))
        seen = set()
        for d, kind, r in deps:
            if d is op or id(d) in seen:
                continue
            if d.eng == op.eng and not d.is_dma and not op.is_dma:
                if op.eng == PE:
                    continue
                if kind == "war" or r.excl:
                    continue
            seen.add(id(d))
            op.deps.append(d)
            d.signal = True
        for r in reads:
            r.readers.append(op)
        for r in writes:
            r.last_w = op
            r.readers = []

    def op(self, eng, fn, reads=(), writes=()):
        o = Op(eng, fn)
        self._add_deps(o, reads, writes)
        self.ops.append(o)
        return o

    def dma(self, fns, reads=(), writes=(), queue=SP, sem_res=None):
        o = Op(queue, fns)
        o.is_dma = True
        o.ndma = len(fns)
        if sem_res is None:
            sem_res = (list(writes) + list(reads))[0]
        sem_res = getattr(sem_res, "base", sem_res)
        o.dres = sem_res
        self._add_deps(o, reads, writes)
        sem_res.dcount += 16 * o.ndma
        o.dtarget = sem_res.dcount
        self.ops.append(o)
        return o

    def barrier(self):
        self.ops.append(("barrier",))

    def emit(self):
        nc = self.nc
        for e in COMPUTE:
            self.esem[e] = nc.alloc_semaphore("sem_" + e)
        nd = 0
        for r in self.res:
            if r.dcount > 0:
                r.dsem = nc.alloc_semaphore("dsem%d" % nd)
                nd += 1
        last_by_eng = {}
        dma_res_tot = {}
        flat = []
        for o in self.ops:
            if isinstance(o, tuple):
                for e, lo in last_by_eng.items():
                    lo.signal = True
                flat.append(("barrier", dict(last_by_eng), dict(dma_res_tot)))
                continue
            flat.append(o)
            if o.is_dma:
                dma_res_tot[id(o.dres)] = (o.dres, o.dtarget)
            else:
                last_by_eng[o.eng] = o
        cnt = {e: 0 for e in COMPUTE}
        for o in flat:
            if isinstance(o, tuple):
                continue
            if not o.is_dma and o.signal:
                cnt[o.eng] += 1
                o.cnt = cnt[o.eng]
        streams = {e: [] for e in (PE, ACT, DVE, POOL, SP)}
        waited = {e: {} for e in streams}

        def need(eng, sem, val, out):
            key = id(sem)
            if waited[eng].get(key, 0) >= val:
                return
            waited[eng][key] = val
            out.append(("wait", sem, val))

        for o in flat:
            if isinstance(o, tuple):
                _, lasts, dtot = o
                for eng in streams:
                    out = streams[eng]
                    for e, lo in lasts.items():
                        if e != eng:
                            need(eng, self.esem[e], lo.cnt, out)
                    for _, (r, tot) in dtot.items():
                        need(eng, r.dsem, tot, out)
                continue
            out = streams[o.eng]
            for d in o.deps:
                if d.is_dma:
                    need(o.eng, d.dres.dsem, d.dtarget, out)
                else:
                    need(o.eng, self.esem[d.eng], d.cnt, out)
            out.append(("op", o))
        self.n_instr = {e: len(s) for e, s in streams.items()}
        self.n_sems = nd + 4
        esem = self.esem

        def run_stream(eng_name):
            def body(eng):
                for item in streams[eng_name]:
                    if item[0] == "wait":
                        eng.wait_ge(item[1], item[2])
                    else:
                        o = item[1]
                        if o.is_dma:
                            for f in o.fn:
                                f(eng).then_inc(o.dres.dsem, 16)
                        else:
                            ins = o.fn(eng)
                            if o.signal:
                                ins.then_inc(esem[o.eng], 1)
            return body

        with nc.Block() as block:
            block.tensor(run_stream(PE))
            block.scalar(run_stream(ACT))
            block.vector(run_stream(DVE))
            block.gpsimd(run_stream(POOL))
            block.sync(run_stream(SP))


D = 2048
NCH = 16
SEQ = 4096
CTX = 256
T = SEQ + CTX
DEPTH = 4
HID = 5632
NHC = 44
IN_W = 5664
EPS = 1e-6
TB = 512
BLOCKS = [(i * TB, TB, False) for i in range(SEQ // TB)] + [(SEQ, CTX, True)]
OFF_Q, OFF_K, OFF_V, OFF_G, OFF_Z, OFF_A, OFF_B = 0, 1024, 1280, 1536, 4608, 5632, 5648


class G:
    pass


def build_program(body=None, dbg=None, mix_input=False):
    nc = bass.Bass("TRN2", target_bir_lowering=False)
    S = Sched(nc)
    g = G()
    g.nc, g.S = nc, S
    dbg = dbg or {}

    def din(name, shape, dt=F32):
        return nc.dram_tensor(name, list(shape), dt, kind="ExternalInput").ap()

    def dscr(name, shape, dt):
        kind = "ExternalOutput" if OPTS.get("dbg_scratch") else "Internal"
        return nc.dram_tensor(name, list(shape), dt, kind=kind).ap()

    g.xin = din("xin", [D, T])
    g.cc = din("cc", [128, NCH * 2])
    g.w_ada = din("w_ada", [DEPTH, D, 6 * D])
    g.b_ada = din("b_ada", [128, DEPTH * 96])
    g.gains = din("gains", [128, 4 * DEPTH * NCH])
    g.w_in = din("w_in", [2, D, IN_W])
    g.w_conv = din("w_conv", [128, 2 * 24 * 5])
    g.sink = din("sink", [1, 16])
    g.alog = din("alog", [16, 2])
    g.dtb = din("dtb", [16, 2])
    g.gnorm = din("gnorm", [1, 256])
    g.w_out = din("w_out", [2, D, D])
    g.w_four = din("w_four", [2, D, D])
    g.w_gate = din("w_gate", [DEPTH, D, HID])
    g.w_up = din("w_up", [DEPTH, D, HID])
    g.w_down = din("w_down", [DEPTH, HID, D])
    g.ropec = din("ropec", [128, SEQ])
    g.ropes = din("ropes", [128, SEQ])
    g.gmask = din("gmask", [128, 7 * 128])
    g.dftn = din("dftn", [2, 256, 256])
    g.dftl = din("dftl", [2, SEQ, SEQ])
    g.dftc = din("dftc", [2, CTX, CTX])
    g.out = nc.dram_tensor("outT", [D, SEQ], F32, kind="ExternalOutput").ap()
    g.xs = dscr("xs", [D, T], F32)
    g.mixT = din("mixT", [D, T], BF16) if mix_input else dscr("mixT", [D, T], BF16)
    g.hT = dscr("hT", [D, T], BF16)
    g.qT = dscr("qT", [1024, T], BF16)
    g.kT = dscr("kT", [256, T], BF16)
    g.vtm = dscr("vtm", [T, 256], BF16)
    g.gqkvT = dscr("gqkvT", [3072, T], BF16)
    g.ztm = dscr("ztm", [T, 1024], BF16)
    g.abT = dscr("abT", [32, T], F32)
    g.dbg_out = {}
    for name, (shape, dt) in dbg.items():
        g.dbg_out[name] = nc.dram_tensor("dbg_" + name, list(shape), dt, kind="ExternalOutput").ap()

    g.ones_bf = S.sb("ones_bf", [128, 128], BF16)
    g.ones_f = S.sb("ones_f", [128, 128], F32)
    g.ident = S.sb("ident", [128, 128], F32)
    g.eps_c = S.sb("eps_c", [128, 1], F32)
    g.mod = S.sb("mod", [128, DEPTH * 6 * NCH * 2], F32)
    g.gn = S.sb("gn", [128, 4 * DEPTH * NCH], F32)
    g.cols = S.sb("cols", [128, DEPTH * 6 * NCH * 2], F32)
    g.pb = [S.ps("pb%d" % i, [128, 512]) for i in range(8)]
    g.stg = [S.sb("stg%d" % i, [128, 2048], F32) for i in range(3)]
    g.wbf = [S.sb("wbf%d" % i, [128, 2048], BF16) for i in range(4)]
    g.stg_i = 0
    g.wbf_i = 0

    S.op(POOL, lambda e: e.memset(g.ones_bf[:, :], 1.0), writes=[g.ones_bf])
    S.op(POOL, lambda e: e.memset(g.ones_f[:, :], 1.0), writes=[g.ones_f])
    S.op(POOL, lambda e: e.memset(g.eps_c[:, :], EPS), writes=[g.eps_c])
    S.op(POOL, lambda e: e.memset(g.ident[:, :], 1.0), writes=[g.ident])
    S.op(POOL, lambda e: e.affine_select(out=g.ident[:, :], in_=g.ident[:, :], pattern=[[-1, 128]],
                                        compare_op=ALU.is_equal, fill=0.0, base=0, channel_multiplier=1),
         reads=[g.ident], writes=[g.ident])
    S.dma([lambda e: e.dma_start(out=g.gn[:, :], in_=g.gains[:, :])], writes=[g.gn])

    (body or full_forward)(g)
    S.barrier()
    S.emit()
    return g


def wload(g, src, kc, ncols):
    S = g.S
    st = g.stg[g.stg_i % len(g.stg)]
    g.stg_i += 1
    bf = g.wbf[g.wbf_i % len(g.wbf)]
    g.wbf_i += 1
    n = kc * ncols
    stv = st[:, 0:n].rearrange("p (a b) -> p a b", a=kc)
    bfv = bf[:, 0:n].rearrange("p (a b) -> p a b", a=kc)
    S.dma([lambda e: e.dma_start(out=stv, in_=src)], writes=[st])
    S.op(POOL, lambda e: e.tensor_copy(out=bf[:, 0:n], in_=st[:, 0:n]), reads=[st], writes=[bf])
    return bf, bfv


def wview(w2d, k0, kc, c0, ncols):
    return w2d.rearrange("(kc p) n -> p kc n", p=128)[:, k0:k0 + kc, c0:c0 + ncols]


def mm(g, pres, pap, lres, lap, rres, rap, start=True, stop=True):
    return g.S.op(PE, lambda e: e.matmul(pap, lhsT=lap, rhs=rap, start=start, stop=stop),
                  reads=[lres, rres], writes=[pres])


def tr(g, pres, pap, ires, iap):
    n = iap.shape[0]
    return g.S.op(PE, lambda e: e.transpose(out=pap, in_=iap, identity=g.ident[0:n, 0:n]),
                  reads=[ires, g.ident], writes=[pres])


def act(g, ores, oap, ires, iap, func, bias=None, scale=None, extra_reads=()):
    kw = {}
    if bias is not None:
        kw["bias"] = bias
    if scale is not None:
        kw["scale"] = scale
    rd = list(ires) if isinstance(ires, (list, tuple)) else [ires]
    return g.S.op(ACT, lambda e: e.activation(out=oap, in_=iap, func=func, **kw),
                  reads=rd + list(extra_reads), writes=[ores])


def tt(g, eng, ores, oap, r0, a0, r1, a1, op):
    return g.S.op(eng, lambda e: e.tensor_tensor(out=oap, in0=a0, in1=a1, op=op),
                  reads=[r0, r1], writes=[ores])


def stt(g, eng, ores, oap, r0, a0, sc, r1, a1, op0, op1, extra_reads=()):
    return g.S.op(eng, lambda e: e.scalar_tensor_tensor(out=oap, in0=a0, scalar=sc, in1=a1, op0=op0, op1=op1),
                  reads=[r0, r1] + list(extra_reads), writes=[ores])


def ts(g, eng, ores, oap, r0, a0, s1, s2, op0, op1=None, extra_reads=()):
    if op1 is None:
        return g.S.op(eng, lambda e: e.tensor_scalar(out=oap, in0=a0, scalar1=s1, scalar2=None, op0=op0),
                      reads=[r0] + list(extra_reads), writes=[ores])
    return g.S.op(eng, lambda e: e.tensor_scalar(out=oap, in0=a0, scalar1=s1, scalar2=s2, op0=op0, op1=op1),
                  reads=[r0] + list(extra_reads), writes=[ores])


def cp(g, eng, ores, oap, ires, iap):
    if eng == ACT:
        return act(g, ores, oap, ires, iap, AF.Copy)
    return g.S.op(eng, lambda e: e.tensor_copy(out=oap, in_=iap), reads=[ires], writes=[ores])


def recip(g, ores, oap, ires, iap):
    return g.S.op(DVE, lambda e: e.reciprocal(out=oap, in_=iap), reads=[ires], writes=[ores])


def dma(g, oap, iap, reads=(), writes=(), queue=SP):
    return g.S.dma([lambda e: e.dma_start(out=oap, in_=iap)], reads=list(reads), writes=list(writes), queue=queue)


def col(g, l, kind, ch, v):
    i = ((l * 6 + kind) * NCH + ch) * 2 + v
    return g.cols[:, i:i + 1]


def phase_mod(g):
    S = g.S
    with g.nc.reset_on_exit():
        sc = S.sb("sc", [128, NCH * 2], F32)
        sg = S.sb("sg", [128, NCH * 2], F32)
        bada = S.sb("bada", [128, DEPTH * 96], F32)
        dma(g, sc[:, :], g.cc[:, :], writes=[sc])
        dma(g, bada[:, :], g.b_ada[:, :], writes=[bada])
        act(g, sg, sg[:, :], sc, sc[:, :], AF.Sigmoid)
        tt(g, DVE, sc, sc[:, :], sc, sc[:, :], sg, sg[:, :], ALU.mult)
        for l in range(DEPTH):
            pb = g.pb[l % 2]
            for nj in range(96):
                st = g.stg[g.stg_i % len(g.stg)]
                g.stg_i += 1
                stv = st[:, :].rearrange("p (a b) -> p a b", a=16)
                dma(g, stv, wview(g.w_ada[l], 0, 16, nj * 128, 128), writes=[st])
                for k in range(16):
                    mm(g, pb, pb[:, nj * 2:nj * 2 + 2], st, stv[:, k, :], sc, sc[:, 2 * k:2 * k + 2],
                       start=(k == 0), stop=(k == 15))
            base = l * 6 * NCH * 2
            mv = g.mod[:, base:base + 192].rearrange("p (n v) -> p n v", v=2)
            pv = pb[:, 0:192].rearrange("p (n v) -> p n v", v=2)
            for v in range(2):
                tt(g, DVE, g.mod, mv[:, :, v], pb, pv[:, :, v], bada, bada[:, l * 96:(l + 1) * 96], ALU.add)
            def mview(j, v):
                return g.mod[:, base + j * 32: base + (j + 1) * 32].rearrange("p (c v) -> p c v", v=2)[:, :, v]

            def cview(kind, v):
                return g.cols[:, base + kind * 32: base + (kind + 1) * 32].rearrange("p (c v) -> p c v", v=2)[:, :, v]

            def gview(which):
                o = (which * DEPTH + l) * NCH
                return g.gn[:, o:o + NCH]

            for v in range(2):
                stt(g, DVE, g.cols, cview(0, v), g.mod, mview(1, v), 1.0, g.gn, gview(0), ALU.add, ALU.mult)
                cp(g, DVE, g.cols, cview(1, v), g.mod, mview(0, v))
                tt(g, DVE, g.cols, cview(2, v), g.mod, mview(2, v), g.gn, gview(1), ALU.mult)
                stt(g, DVE, g.cols, cview(3, v), g.mod, mview(4, v), 1.0, g.gn, gview(2), ALU.add, ALU.mult)
                cp(g, DVE, g.cols, cview(4, v), g.mod, mview(3, v))
                tt(g, DVE, g.cols, cview(5, v), g.mod, mview(5, v), g.gn, gview(3), ALU.mult)
        S.barrier()


class RowTiles:
    def __init__(self, g, with_ffn=True):
        S = g.S
        self.xt = S.sb("xt", [128, NCH * TB], F32)
        self.xc = [S.sub("xt%d" % j, self.xt[:, j * TB:(j + 1) * TB]) for j in range(NCH)]
        self.yt = S.sb("yt", [128, NCH * TB], F32)
        self.yc = [S.sub("yt%d" % j, self.yt[:, j * TB:(j + 1) * TB]) for j in range(NCH)]
        self.ht = S.sb("ht", [128, NCH * TB], BF16)
        self.hc = [S.sub("ht%d" % j, self.ht[:, j * TB:(j + 1) * TB]) for j in range(NCH)]
        if with_ffn:
            self.at = S.sb("at", [128, NHC * TB], BF16)
            self.ac = [S.sub("at%d" % j, self.at[:, j * TB:(j + 1) * TB]) for j in range(NHC)]
            self.mc = [self.ac[j] for j in range(NCH)]
        self.sq = [S.sb("sq%d" % i, [128, TB], BF16) for i in range(2)]
        self.tmp = [S.sb("tmp%d" % i, [128, TB], F32) for i in range(2)]
        self.rstd = S.sb("rstd", [128, TB], F32)
        self.i = 0


def ssq_step(g, rt, src_res, src_ap, w, first, last):
    sq = rt.sq[rt.i % 2]
    rt.i += 1
    act(g, sq, sq[:, 0:w], src_res, src_ap, AF.Square)
    mm(g, g.pb[2], g.pb[2][:, 0:w], g.ones_bf, g.ones_bf[:, :], sq, sq[:, 0:w], start=first, stop=last)


def rstd_finish(g, rt, w, nfeat):
    act(g, rt.rstd, rt.rstd[:, 0:w], g.pb[2], g.pb[2][:, 0:w], AF.Sqrt, bias=g.eps_c[:, 0:1], scale=1.0 / nfeat,
        extra_reads=[g.eps_c])
    recip(g, rt.rstd, rt.rstd[:, 0:w], rt.rstd, rt.rstd[:, 0:w])


def load_x(g, rt, src, t0, w):
    xv = rt.xt[:, :].rearrange("p (c t) -> p c t", c=NCH)[:, :, 0:w]
    dma(g, xv, src.rearrange("(c p) t -> p c t", p=128)[:, :, t0:t0 + w], writes=rt.xc)


def prenorm(g, rt, l, kinds, v, w):
    kg, ksh = kinds
    for j in range(NCH):
        ssq_step(g, rt, rt.xc[j], rt.xc[j][:, 0:w], w, j == 0, j == NCH - 1)
    rstd_finish(g, rt, w, D)
    for j in range(NCH):
        tmp = rt.tmp[j % 2]
        tt(g, DVE, tmp, tmp[:, 0:w], rt.xc[j], rt.xc[j][:, 0:w], rt.rstd, rt.rstd[:, 0:w], ALU.mult)
        act(g, rt.hc[j], rt.hc[j][:, 0:w], tmp, tmp[:, 0:w], AF.Identity,
            bias=col(g, l, ksh, j, v), scale=col(g, l, kg, j, v), extra_reads=[g.cols])


def resid_update(g, rt, l, kind_gg, v, w):
    rstd_finish(g, rt, w, D)
    for j in range(NCH):
        tmp = rt.tmp[j % 2]
        stt(g, DVE, tmp, tmp[:, 0:w], rt.yc[j], rt.yc[j][:, 0:w], col(g, l, kind_gg, j, v),
            rt.rstd, rt.rstd[:, 0:w], ALU.mult, ALU.mult, extra_reads=[g.cols])
        tt(g, DVE, rt.xc[j], rt.xc[j][:, 0:w], rt.xc[j], rt.xc[j][:, 0:w], tmp, tmp[:, 0:w], ALU.add)


def proj_to_yt(g, rt, w2d, kchunks, rhs_res, w):
    nsl = (kchunks + 15) // 16
    for j in range(NCH):
        pb = g.pb[j % 2]
        for s in range(nsl):
            k0 = s * 16
            kc = min(16, kchunks - k0)
            bf, bfv = wload(g, wview(w2d, k0, kc, j * 128, 128), kc, 128)
            for k in range(kc):
                kk = k0 + k
                mm(g, pb, pb[:, 0:w], bf, bfv[:, k, :], rhs_res[kk], rhs_res[kk][:, 0:w],
                   start=(kk == 0), stop=(kk == kchunks - 1))
        cp(g, ACT, rt.yc[j], rt.yc[j][:, 0:w], pb, pb[:, 0:w])
        ssq_step(g, rt, pb, pb[:, 0:w], w, j == 0, j == NCH - 1)


def phase_post(g, rt, l, wmix2d, x_src, blocks, final):
    for (t0, w, is_ctx) in blocks:
        v = 1 if is_ctx else 0
        load_x(g, rt, x_src, t0, w)
        mv = rt.at[:, 0:NCH * TB].rearrange("p (c t) -> p c t", c=NCH)[:, :, 0:w]
        dma(g, mv, g.mixT.rearrange("(c p) t -> p c t", p=128)[:, :, t0:t0 + w], writes=rt.mc)
        proj_to_yt(g, rt, wmix2d, NCH, rt.mc, w)
        resid_update(g, rt, l, 2, v, w)
        prenorm(g, rt, l, (3, 4), v, w)
        for n in range(NHC):
            pg = g.pb[3 + (n % 2) * 2]
            pu = g.pb[4 + (n % 2) * 2]
            bg, bgv = wload(g, wview(g.w_gate[l], 0, 16, n * 128, 128), 16, 128)
            for k in range(16):
                mm(g, pg, pg[:, 0:w], bg, bgv[:, k, :], rt.hc[k], rt.hc[k][:, 0:w], start=(k == 0), stop=(k == 15))
            bu, buv = wload(g, wview(g.w_up[l], 0, 16, n * 128, 128), 16, 128)
            for k in range(16):
                mm(g, pu, pu[:, 0:w], bu, buv[:, k, :], rt.hc[k], rt.hc[k][:, 0:w], start=(k == 0), stop=(k == 15))
            tmp = rt.tmp[n % 2]
            act(g, tmp, tmp[:, 0:w], pg, pg[:, 0:w], AF.Silu)
            tt(g, DVE, rt.ac[n], rt.ac[n][:, 0:w], tmp, tmp[:, 0:w], pu, pu[:, 0:w], ALU.mult)
        proj_to_yt(g, rt, g.w_down[l], NHC, rt.ac, w)
        resid_update(g, rt, l, 5, v, w)
        xv = rt.xt[:, :].rearrange("p (c t) -> p c t", c=NCH)[:, :, 0:w]
        if final and not is_ctx:
            dst = g.out.rearrange("(c p) t -> p c t", p=128)[:, :, t0:t0 + w]
        else:
            dst = g.xs.rearrange("(c p) t -> p c t", p=128)[:, :, t0:t0 + w]
        dma(g, dst, xv, reads=rt.xc, queue=ACT)


def phase_pre_even(g, rt, l, x_src, blocks):
    S = g.S
    e_i = l // 2
    w2d = g.w_in[e_i]
    rope_c = S.sb("rope_c", [128, TB], F32)
    rope_s = S.sb("rope_s", [128, TB], F32)
    pswap = S.sb("pswap", [128, 128], BF16)
    qraw = [S.sb("qraw%d" % i, [128, TB], BF16) for i in range(2)]
    stage = [S.sb("stage%d" % i, [128, TB], BF16) for i in range(2)]
    stage32 = [S.sb("stage32_%d" % i, [128, TB], F32) for i in range(2)]
    t1 = [S.sb("rt1_%d" % i, [128, TB], F32) for i in range(2)]
    pf = S.sb("pswapf", [128, 128], F32)
    S.op(POOL, lambda e: e.memset(pf[:, :], 0.0), writes=[pf])
    cp(g, DVE, pf, pf[:, 0:64], g.ident, g.ident[:, 64:128])
    cp(g, DVE, pf, pf[:, 64:128], g.ident, g.ident[:, 0:64])
    cp(g, DVE, pswap, pswap[:, :], pf, pf[:, :])
    si = 0
    for (t0, w, is_ctx) in blocks:
        v = 1 if is_ctx else 0
        load_x(g, rt, x_src, t0, w)
        if not is_ctx:
            dma(g, rope_c[:, 0:w], g.ropec[:, t0:t0 + w], writes=[rope_c])
            dma(g, rope_s[:, 0:w], g.ropes[:, t0:t0 + w], writes=[rope_s])
        prenorm(g, rt, l, (0, 1), v, w)
        fm_chunks = [("q", i, OFF_Q + i * 128, 128) for i in range(8)] + \
                    [("k", i, OFF_K + i * 128, 128) for i in range(2)] + \
                    [("g", i, OFF_G + i * 128, 128) for i in range(24)] + [("ab", 0, OFF_A, 32)]
        parts = OPTS.get("pre_parts", "qkgat")
        fm_chunks = [f for f in fm_chunks if f[0][0] in parts]
        for ci, (kind, idx, c0, ncol) in enumerate(fm_chunks):
            pb = g.pb[ci % 2]
            bf, bfv = wload(g, wview(w2d, 0, 16, c0, ncol), 16, ncol)
            for k in range(16):
                mm(g, pb, pb[0:ncol, 0:w], bf, bfv[:, k, :], rt.hc[k], rt.hc[k][:, 0:w], start=(k == 0), stop=(k == 15))
            if kind in ("q", "k"):
                dst = (g.qT if kind == "q" else g.kT)[idx * 128:(idx + 1) * 128, t0:t0 + w]
                st = stage[si % 2]
                if is_ctx or OPTS.get("norope"):
                    cp(g, ACT, st, st[:, 0:w], pb, pb[:, 0:w])
                else:
                    a1 = t1[si % 2]
                    prot = g.pb[3 + si % 2]
                    for k in range(16):
                        mm(g, prot, prot[0:64, 0:w], bf, bfv[:, k, 64:128], rt.hc[k], rt.hc[k][:, 0:w], start=(k == 0), stop=(k == 15))
                    for k in range(16):
                        mm(g, prot, prot[64:128, 0:w], bf, bfv[:, k, 0:64], rt.hc[k], rt.hc[k][:, 0:w], start=(k == 0), stop=(k == 15))
                    tt(g, DVE, a1, a1[:, 0:w], pb, pb[:, 0:w], rope_c, rope_c[:, 0:w], ALU.mult)
                    s32 = stage32[si % 2]
                    tt(g, DVE, s32, s32[:, 0:w], prot, prot[:, 0:w], rope_s, rope_s[:, 0:w], ALU.mult)
                    tt(g, DVE, st, st[:, 0:w], a1, a1[:, 0:w], s32, s32[:, 0:w], ALU.add)
                dma(g, dst, st[:, 0:w], reads=[st], queue=ACT)
                si += 1
            elif kind == "g":
                st = stage[si % 2]
                cp(g, ACT, st, st[:, 0:w], pb, pb[:, 0:w])
                dma(g, g.gqkvT[idx * 128:(idx + 1) * 128, t0:t0 + w], st[:, 0:w], reads=[st], queue=ACT)
                si += 1
            else:
                s32 = stage32[si % 2]
                cp(g, ACT, s32, s32[0:32, 0:w], pb, pb[0:32, 0:w])
                dma(g, g.abT[:, t0:t0 + w], s32[0:32, 0:w], reads=[s32], queue=ACT)
                si += 1
        nsub = w // 128
        tm_slabs = [("v", OFF_V + i * 128, i) for i in range(2)] + [("z", OFF_Z + i * 128, i) for i in range(8)]
        if "t" not in parts:
            tm_slabs = []
        for (kind, c0, idx) in tm_slabs:
            bf, bfv = wload(g, wview(w2d, 0, 16, c0, 128), 16, 128)
            pb = g.pb[5 + si % 2]
            for s in range(nsub):
                for k in range(16):
                    mm(g, pb, pb[:, s * 128:(s + 1) * 128], rt.hc[k], rt.hc[k][:, s * 128:(s + 1) * 128], bf, bfv[:, k, :],
                       start=(k == 0), stop=(k == 15))
            st = stage[si % 2]
            cp(g, ACT, st, st[:, 0:nsub * 128], pb, pb[:, 0:nsub * 128])
            dt_ = g.vtm if kind == "v" else g.ztm
            dst = dt_[t0:t0 + w, idx * 128:(idx + 1) * 128].rearrange("(s p) c -> p s c", p=128)
            dma(g, dst, st[:, 0:nsub * 128].rearrange("p (s c) -> p s c", c=128), reads=[st], queue=ACT)
            si += 1


def phase_attn(g, l, with_ctx_q):
    S = g.S
    e_i = l // 2
    kt = S.sb("a_kt", [128, T], BF16)
    vt = S.sb("a_vt", [128, 34 * 128], BF16)
    qt = [S.sb("a_qt%d" % i, [128, 4 * TB], BF16) for i in range(2)]
    ost = [S.sb("a_ost%d" % i, [128, 4 * TB], BF16) for i in range(2)]
    eb = [S.sb("a_eb%d" % i, [128, 512], BF16) for i in range(3)]
    ef = [S.sb("a_ef%d" % i, [128, 512], F32) for i in range(2)]
    mprev = S.sb("a_mprev", [128, 512], F32)
    mnext = S.sb("a_mnext", [128, 512], F32)
    skb = S.sb("a_skb", [128, 16], F32)
    esk = S.sb("a_esk", [128, 512], F32)
    rden = [S.sb("a_rden%d" % i, [128, 512], F32) for i in range(2)]
    for mt, cm in ((mprev, 1), (mnext, -1)):
        S.op(POOL, lambda e, mt=mt: e.memset(mt[:, :], 1.0), writes=[mt])
        S.op(POOL, lambda e, mt=mt, cm=cm: e.affine_select(
            out=mt[:, :].rearrange("p (g q) -> p g q", g=4), in_=mt[:, :].rearrange("p (g q) -> p g q", g=4),
            pattern=[[0, 4], [-cm, 128]], compare_op=ALU.is_ge, fill=0.0, base=0, channel_multiplier=cm),
            reads=[mt], writes=[mt])
    dma(g, skb[:, :], g.sink[0:1, :].partition_broadcast(128), writes=[skb])
    act(g, skb, skb[:, :], skb, skb[:, :], AF.Exp)
    scale = 128.0 ** -0.5
    gi = 0
    ei = 0
    for hk in range(2):
        dma(g, kt[:, :], g.kT[hk * 128:(hk + 1) * 128, :], writes=[kt])
        vtv = vt[:, :].rearrange("p (c d) -> p c d", d=128)
        vsrc = g.vtm[:, hk * 128:(hk + 1) * 128].rearrange("(c p) d -> p c d", p=128)
        S.dma([(lambda e, o_=vtv[:, a:a + 6, :], i_=vsrc[:, a:a + 6, :]: e.dma_start(out=o_, in_=i_)) for a in range(0, 30, 6)]
              + [lambda e, o_=vtv[:, 30:34, :], i_=vsrc[:, 30:34, :]: e.dma_start(out=o_, in_=i_)], writes=[vt])
        for gq in range(4):
            h = e_i * 8 + hk * 4 + gq
            ts(g, DVE, esk, esk[:, gq * 128:(gq + 1) * 128], g.ones_f, g.ones_f[:, :], skb[:, h:h + 1], None, ALU.mult,
               extra_reads=[skb])
        groups = [(i * TB, 4, False) for i in range(SEQ // TB)]
        if with_ctx_q:
            groups.append((SEQ, 2, True))
        for (t0, nqb, is_ctx) in groups:
            W = nqb * 128
            q = qt[gi % 2]
            o = ost[gi % 2]
            gi += 1
            qv = q[:, :].rearrange("p (g t) -> p g t", g=4)
            ov = o[:, :].rearrange("p (g t) -> p g t", g=4)
            dma(g, qv[:, :, 0:W], g.qT[hk * 512:(hk + 1) * 512, t0:t0 + W].rearrange("(g p) t -> p g t", p=128), writes=[q])
            for qi in range(nqb):
                qb = t0 // 128 + qi
                if is_ctx:
                    keys = [(32, None), (33, None)]
                else:
                    keys = []
                    if qb > 0:
                        keys.append((qb - 1, mprev))
                    keys.append((qb, None))
                    if qb < 31:
                        keys.append((qb + 1, mnext))
                    keys += [(32, None), (33, None)]
                po = g.pb[3 + qi % 2]
                pd = g.pb[5 + qi % 2]
                for i, (kc, m) in enumerate(keys):
                    ps = g.pb[i % 3]
                    mm(g, ps, ps[:, :].rearrange("p (g q) -> p g q", g=4), kt, kt[:, kc * 128:(kc + 1) * 128],
                       q, qv[:, :, qi * 128:(qi + 1) * 128])
                    e_ = eb[ei % 3]
                    if m is not None:
                        f_ = ef[ei % 2]
                        act(g, f_, f_[:, :], ps, ps[:, :], AF.Exp, scale=scale)
                        tt(g, DVE, e_, e_[:, :], f_, f_[:, :], m, m[:, :], ALU.mult)
                    else:
                        act(g, e_, e_[:, :], ps, ps[:, :], AF.Exp, scale=scale)
                    ei += 1
                    mm(g, po, po[:, :], vt, vt[:, kc * 128:(kc + 1) * 128], e_, e_[:, :], start=(i == 0), stop=(i == len(keys) - 1))
                    mm(g, pd, pd[:, :], g.ones_bf, g.ones_bf[:, :], e_, e_[:, :], start=(i == 0), stop=(i == len(keys) - 1))
                rd = rden[qi % 2]
                tt(g, DVE, rd, rd[:, :], pd, pd[:, :], esk, esk[:, :], ALU.add)
                recip(g, rd, rd[:, :], rd, rd[:, :])
                tt(g, DVE, o, ov[:, :, qi * 128:(qi + 1) * 128], po, po[:, :].rearrange("p (g q) -> p g q", g=4),
                   rd, rd[:, :].rearrange("p (g q) -> p g q", g=4), ALU.mult)
            dma(g, g.mixT[hk * 512:(hk + 1) * 512, t0:t0 + W].rearrange("(g p) t -> p g t", p=128), ov[:, :, 0:W],
                reads=[o], queue=ACT)


class GdnStop(Exception):
    pass


def phase_gdn(g, l):
    try:
        _phase_gdn(g, l)
    except GdnStop:
        g.S.barrier()


def _phase_gdn(g, l):
    S = g.S
    nc = g.nc
    e_i = l // 2
    NC_ = T // 128
    g_tm = S.sb("g_tm", [128, NC_ * 16 + 4], F32)
    b_tm = S.sb("b_tm", [128, NC_ * 16 + 4], F32)
    S.op(POOL, lambda e: e.memset(g_tm[:, :], 0.0), writes=[g_tm])
    pq = [[View(g.pb[b], g.pb[b][:, q * 128:(q + 1) * 128]) for q in range(4)] for b in range(8)]
    with nc.reset_on_exit():
        PW = T // 2
        aT = S.sb("gd_aT", [16, PW], F32)
        bT = S.sb("gd_bT", [16, PW], F32)
        al = S.sb("gd_al", [16, 2], F32)
        db = S.sb("gd_db", [16, 2], F32)
        dma(g, al[:, :], g.alog[:, :], writes=[al])
        dma(g, db[:, :], g.dtb[:, :], writes=[db])
        act(g, al, al[:, :], al, al[:, :], AF.Exp)
        ts(g, DVE, al, al[:, :], al, al[:, :], -1.0, None, ALU.mult)
        for pi in range(2):
            c0 = pi * PW
            dma(g, aT[:, :], g.abT[0:16, c0:c0 + PW], writes=[aT])
            dma(g, bT[:, :], g.abT[16:32, c0:c0 + PW], writes=[bT])
            act(g, aT, aT[:, :], aT, aT[:, :], AF.Exp, bias=db[:, e_i:e_i + 1], extra_reads=[db])
            act(g, aT, aT[:, :], aT, aT[:, :], AF.Ln, bias=g.ones_f[0:16, 0:1], extra_reads=[g.ones_f])
            ts(g, DVE, aT, aT[:, :], aT, aT[:, :], al[:, e_i:e_i + 1], None, ALU.mult, extra_reads=[al])
            act(g, bT, bT[:, :], bT, bT[:, :], AF.Exp, scale=-1.0)
            ts(g, DVE, bT, bT[:, :], bT, bT[:, :], 1.0, None, ALU.add)
            recip(g, bT, bT[:, :], bT, bT[:, :])
            ncp = PW // 128
            for (src, dst, bank) in ((aT, g_tm, 0), (bT, b_tm, 1)):
                pbk = g.pb[bank]
                for cc in range(ncp):
                    tr(g, pbk, pbk[:, cc * 16:(cc + 1) * 16], src, src[:, cc * 128:(cc + 1) * 128])
                cch = c0 // 128
                cp(g, ACT, dst, dst[:, cch * 16:(cch + ncp) * 16], pbk, pbk[:, 0:ncp * 16])
        S.barrier()
    if OPTS.get("gdn_stop", 9) <= 1:
        return
    wcv = S.sb("gd_wcv", [128, 2 * 24 * 5], F32)
    dma(g, wcv[:, :], g.w_conv[:, :], writes=[wcv])
    gain = S.sb("gd_gain", [128, 128], F32)
    dma(g, gain[:, :], g.gnorm[0:1, e_i * 128:(e_i + 1) * 128].partition_broadcast(128), writes=[gain])
    Lm = [S.sb("gd_L%d" % d, [128, 128], F32) for d in range(2)]
    Sm = [S.sb("gd_S%d" % d, [128, 128], F32) for d in range(2)]
    for (mt, cm, base) in ((Lm[0], -1, 0), (Lm[1], 1, 0), (Sm[0], 1, -1), (Sm[1], -1, -1)):
        S.op(POOL, lambda e, mt=mt: e.memset(mt[:, :], 1.0), writes=[mt])
        S.op(POOL, lambda e, mt=mt, cm=cm, base=base: e.affine_select(
            out=mt[:, :], in_=mt[:, :], pattern=[[-cm, 128]], compare_op=ALU.is_ge, fill=0.0, base=base,
            channel_multiplier=cm), reads=[mt], writes=[mt])
    gmk = S.sb("gd_gmk", [128, 7 * 128], F32)
    dma(g, gmk[:, :], g.gmask[:, :], writes=[gmk])
    raw = [S.sb("gd_raw%d" % i, [128, T], BF16) for i in range(2)]
    acc = S.sb("gd_acc", [128, T], F32)
    qTn = S.sb("gd_qTn", [128, T], F32)
    kTn = S.sb("gd_kTn", [128, T], F32)
    ktm = S.sb("gd_ktm", [128, T], F32)
    vtm = S.sb("gd_vtm", [128, T], BF16)
    oacc = S.sb("gd_oacc", [128, T], F32)
    sqb = [S.sb("gd_sq%d" % i, [128, 512], BF16) for i in range(2)]
    rn = S.sb("gd_rn", [128, 512], F32)
    ssq = S.sb("gd_ssq", [128, NC_], F32)
    t128 = [S.sb("gd_t%d" % i, [128, 128], F32) for i in range(2)]

    class DirT:
        pass
    DT_ = []
    for d in range(2):
        t_ = DirT()
        for nm in ("G1", "tpos", "tneg", "Dm", "DTt", "egcb", "A0", "A1", "At0", "At1", "R0", "R1", "AT", "vb", "kbg",
                   "kdec", "qd", "wT", "usb", "vnew", "St", "Ab", "Eb", "Tb", "Xp"):
            setattr(t_, nm, S.sb("gd_%s%d" % (nm, d), [128, 128], F32))
        cols_t = S.sb("gd_cols%d" % d, [128, 8], F32)
        t_.c = [S.sub("gd_c%d_%d" % (d, i), cols_t[:, i:i + 1]) for i in range(8)]
        t_.pi = 0
        DT_.append(t_)

    def pslot(d):
        t_ = DT_[d]
        i = t_.pi % 4
        t_.pi += 1
        return pq[4 * d + i][0]

    rri = [0]

    def evac(ores, oap, ires, iap):
        rri[0] += 1
        cp(g, ACT if rri[0] % 2 else DVE, ores, oap, ires, iap)

    def chk(k):
        if OPTS.get("gdn_chk", 99) <= k:
            raise GdnStop()

    wbase = e_i * 24 * 5
    for h in range(OPTS.get("gdn_heads", 8)):
        for part in range(3):
            ch = part * 8 + h
            rw = raw[part % 2]
            dma(g, rw[:, :], g.gqkvT[ch * 128:(ch + 1) * 128, :], writes=[rw])

            def wc(j):
                i = wbase + ch * 5 + j
                return wcv[:, i:i + 1]
            act(g, acc, acc[:, :], rw, rw[:, :], AF.Copy, scale=wc(2), extra_reads=[wcv])
            ti = 0
            for j in (0, 1, 3, 4):
                s_ = j - 2
                for (b0, ln) in ((0, SEQ), (SEQ, CTX)):
                    lo = b0 + max(0, -s_)
                    hi = b0 + min(ln, ln - s_)
                    stt(g, DVE, acc, acc[:, lo:hi], rw, rw[:, lo + s_:hi + s_], wc(j),
                        acc, acc[:, lo:hi], ALU.mult, ALU.add, extra_reads=[wcv])
                    ti += 1
            act(g, acc, acc[:, :], acc, acc[:, :], AF.Silu)
            if part < 2:
                dst = qTn if part == 0 else kTn
                sc_ = (128.0 ** -0.5) if part == 0 else 1.0
                for bi in range((T + 511) // 512):
                    c0 = bi * 512
                    w = min(512, T - c0)
                    sq = sqb[bi % 2]
                    pbk = g.pb[bi % 2]
                    act(g, sq, sq[:, 0:w], acc, acc[:, c0:c0 + w], AF.Square)
                    mm(g, pbk, pbk[:, 0:w], g.ones_bf, g.ones_bf[:, :], sq, sq[:, 0:w])
                    act(g, rn, rn[:, 0:w], pbk, pbk[:, 0:w], AF.Sqrt, bias=g.eps_c[:, 0:1], scale=1.0, extra_reads=[g.eps_c])
                    recip(g, rn, rn[:, 0:w], rn, rn[:, 0:w])
                    stt(g, DVE, dst, dst[:, c0:c0 + w], acc, acc[:, c0:c0 + w], sc_, rn, rn[:, 0:w], ALU.mult, ALU.mult)
            if part >= 1:
                src = kTn if part == 1 else acc
                dstm = ktm if part == 1 else vtm
                for c4 in range(0, NC_, 4):
                    n4 = min(4, NC_ - c4)
                    pbk = g.pb[2 + (c4 // 4) % 2]
                    for cc in range(n4):
                        c = c4 + cc
                        tr(g, pbk, pbk[:, cc * 128:(cc + 1) * 128], src, src[:, c * 128:(c + 1) * 128])
                    evac(dstm, dstm[:, c4 * 128:(c4 + n4) * 128], pbk, pbk[:, 0:n4 * 128])
        if OPTS.get("gdn_stop", 9) <= 2:
            return
        S.op(POOL, lambda e: e.memset(oacc[:, :], 0.0), writes=[oacc])
        for d in range(2):
            S.op(POOL, lambda e, d=d: e.memset(DT_[d].St[:, :], 0.0), writes=[DT_[d].St])
        S.barrier()
        order = [[32, 33] + list(range(32)), [33, 32] + list(range(31, -1, -1))]
        for step in range(OPTS.get("gdn_steps", NC_)):
            for d in range(2):
                c = order[d][step]
                t_ = DT_[d]
                colid = c * 16 + d * 8 + h
                gcol = g_tm[:, colid:colid + 1]
                bcol = b_tm[:, colid:colid + 1]
                last = 127 if d == 0 else 0
                cs = slice(c * 128, (c + 1) * 128)
                gccol, egcol, glc, kdsc, nbeta, bg = t_.c[0], t_.c[1], t_.c[2], t_.c[3], t_.c[4], t_.c[5]
                ts(g, DVE, t_.G1, t_.G1[:, :], g.ones_f, g.ones_f[:, :], gcol, None, ALU.mult, extra_reads=[g_tm])
                pgcb = pslot(d)
                mm(g, pgcb, pgcb[:, :], t_.G1, t_.G1[:, :], Lm[d], Lm[d][:, :])
                pgcc = pslot(d)
                mm(g, pgcc, pgcc[:, 0:2], Lm[d], Lm[d][:, :], g_tm, g_tm[:, colid:colid + 2])
                cp(g, ACT, gccol, gccol[:, :], pgcc, pgcc[:, 0:1])
                chk(1)
                ts(g, DVE, t_.tpos, t_.tpos[:, :], pgcb, pgcb[:, :], gccol[:, :], 0.0, ALU.subtract, ALU.max, extra_reads=[gccol])
                ts(g, DVE, t_.tneg, t_.tneg[:, :], pgcb, pgcb[:, :], gccol[:, :], 0.0, ALU.subtract, ALU.min, extra_reads=[gccol])
                act(g, t_.Dm, t_.Dm[:, :], t_.tpos, t_.tpos[:, :], AF.Exp, scale=-1.0)
                act(g, t_.DTt, t_.DTt[:, :], t_.tneg, t_.tneg[:, :], AF.Exp)
                act(g, t_.egcb, t_.egcb[:, :], pgcb, pgcb[:, :], AF.Exp)
                act(g, egcol, egcol[:, :], gccol, gccol[:, :], AF.Exp)
                cp(g, ACT, glc, glc[:, :], pgcb, pgcb[:, last:last + 1])
                act(g, kdsc, kdsc[:, :], gccol, gccol[:, :], AF.Exp, bias=glc[:, :], scale=-1.0, extra_reads=[glc])
                chk(2)
                tt(g, POOL, t_.Dm, t_.Dm[:, :], t_.Dm, t_.Dm[:, :], Sm[d], Sm[d][:, :], ALU.mult)
                tt(g, POOL, t_.DTt, t_.DTt[:, :], t_.DTt, t_.DTt[:, :], Lm[d], Lm[d][:, :], ALU.mult)
                ts(g, DVE, nbeta, nbeta[:, :], b_tm, bcol, -1.0, None, ALU.mult)
                tt(g, DVE, bg, bg[:, :], b_tm, bcol, egcol, egcol[:, :], ALU.mult)
                chk(3)
                pG = pslot(d)
                cp(g, DVE, t_.G1, t_.G1[:, :], kTn, kTn[:, cs])
                mm(g, pG, pG[:, :], kTn, kTn[:, cs], t_.G1, t_.G1[:, :])
                chk(3.2)
                pAT = pslot(d)
                mm(g, pAT, pAT[:, :], kTn, kTn[:, cs], qTn, qTn[:, cs])
                chk(3.4)
                tt(g, DVE, t_.A0, t_.A0[:, :], pG, pG[:, :], t_.Dm, t_.Dm[:, :], ALU.mult)
                ts(g, DVE, t_.A0, t_.A0[:, :], t_.A0, t_.A0[:, :], nbeta[:, :], None, ALU.mult, extra_reads=[nbeta])
                chk(3.6)
                tt(g, DVE, t_.AT, t_.AT[:, :], pAT, pAT[:, :], t_.DTt, t_.DTt[:, :], ALU.mult)
                chk(4)
                pT = pslot(d)
                tr(g, pT, pT[:, :], t_.A0, t_.A0[:, :])
                chk(4.2)
                bd = gmk[:, 0:128]
                tt(g, POOL, t_.Ab, t_.Ab[:, :], t_.A0, t_.A0[:, :], gmk, bd, ALU.mult)
                tt(g, DVE, t_.At0, t_.At0[:, :], pT, pT[:, :], gmk, bd, ALU.mult)
                chk(4.4)
                A = [t_.Ab, t_.A1]
                At = [t_.At0, t_.At1]
                R = [t_.R0, t_.R1]
                tt(g, DVE, R[0], R[0][:, :], At[0], At[0][:, :], g.ident, g.ident[:, :], ALU.add)
                chk(5)
                ri = 0
                for j in range(3):
                    a, b = j % 2, (j + 1) % 2
                    pA = pslot(d)
                    mm(g, pA, pA[:, :], At[a], At[a][:, :], A[a], A[a][:, :])
                    if j < 2:
                        pAt = pslot(d)
                        mm(g, pAt, pAt[:, :], A[a], A[a][:, :], At[a], At[a][:, :])
                    evac(A[b], A[b][:, :], pA, pA[:, :])
                    if j < 2:
                        evac(At[b], At[b][:, :], pAt, pAt[:, :])
                    pR = pslot(d)
                    mm(g, pR, pR[:, :], A[b], A[b][:, :], R[ri], R[ri][:, :])
                    tt(g, DVE, R[1 - ri], R[1 - ri][:, :], pR, pR[:, :], R[ri], R[ri][:, :], ALU.add)
                    ri = 1 - ri
                for lv in range(3):
                    off = gmk[:, (1 + 3 * d + lv) * 128:(2 + 3 * d + lv) * 128]
                    P = R[ri]
                    pTb = pslot(d)
                    tr(g, pTb, pTb[:, :], P, P[:, :])
                    tt(g, POOL, t_.Eb, t_.Eb[:, :], t_.A0, t_.A0[:, :], gmk, off, ALU.mult)
                    evac(t_.Tb, t_.Tb[:, :], pTb, pTb[:, :])
                    pX = pslot(d)
                    mm(g, pX, pX[:, :], t_.Eb, t_.Eb[:, :], P, P[:, :])
                    evac(t_.Xp, t_.Xp[:, :], pX, pX[:, :])
                    pY = pslot(d)
                    mm(g, pY, pY[:, :], t_.Tb, t_.Tb[:, :], t_.Xp, t_.Xp[:, :])
                    tt(g, DVE, R[1 - ri], R[1 - ri][:, :], pY, pY[:, :], P, P[:, :], ALU.add)
                    ri = 1 - ri
                Rf = R[ri]
                chk(6)
                ts(g, DVE, t_.vb, t_.vb[:, :], vtm, vtm[:, cs], bcol, None, ALU.mult, extra_reads=[b_tm])
                ts(g, DVE, t_.kbg, t_.kbg[:, :], ktm, ktm[:, cs], bg[:, :], None, ALU.mult, extra_reads=[bg])
                ts(g, DVE, t_.kdec, t_.kdec[:, :], ktm, ktm[:, cs], kdsc[:, :], None, ALU.mult, extra_reads=[kdsc])
                tt(g, DVE, t_.qd, t_.qd[:, :], qTn, qTn[:, cs], t_.egcb, t_.egcb[:, :], ALU.mult)
                pu = pslot(d)
                mm(g, pu, pu[:, :], Rf, Rf[:, :], t_.vb, t_.vb[:, :])
                pw = pslot(d)
                mm(g, pw, pw[:, :], t_.kbg, t_.kbg[:, :], Rf, Rf[:, :])
                evac(t_.usb, t_.usb[:, :], pu, pu[:, :])
                evac(t_.wT, t_.wT[:, :], pw, pw[:, :])
                chk(7)
                pws = pslot(d)
                mm(g, pws, pws[:, :], t_.wT, t_.wT[:, :], t_.St, t_.St[:, :])
                tt(g, DVE, t_.vnew, t_.vnew[:, :], t_.usb, t_.usb[:, :], pws, pws[:, :], ALU.subtract)
                po = pslot(d)
                mm(g, po, po[:, :], t_.qd, t_.qd[:, :], t_.St, t_.St[:, :], start=True, stop=False)
                mm(g, po, po[:, :], t_.AT, t_.AT[:, :], t_.vnew, t_.vnew[:, :], start=False, stop=True)
                tt(g, DVE, oacc, oacc[:, cs], oacc, oacc[:, cs], po, po[:, :], ALU.add)
                pkv = pslot(d)
                mm(g, pkv, pkv[:, :], t_.kdec, t_.kdec[:, :], t_.vnew, t_.vnew[:, :])
                stt(g, DVE, t_.St, t_.St[:, :], t_.St, t_.St[:, :], t_.egcb[:, last:last + 1], pkv, pkv[:, :],
                    ALU.mult, ALU.add, extra_reads=[t_.egcb])
        S.barrier()
        if OPTS.get("gdn_stop", 9) <= 3:
            return
        zt = raw[0]
        yT = raw[1]
        ztv = zt[:, :].rearrange("p (c d) -> p c d", d=128)
        zsrc = g.ztm[:, h * 128:(h + 1) * 128].rearrange("(c p) d -> p c d", p=128)
        S.dma([(lambda e, o_=ztv[:, a:a + 6, :], i_=zsrc[:, a:a + 6, :]: e.dma_start(out=o_, in_=i_)) for a in range(0, 30, 6)]
              + [lambda e, o_=ztv[:, 30:34, :], i_=zsrc[:, 30:34, :]: e.dma_start(out=o_, in_=i_)], writes=[zt])
        act(g, acc, acc[:, :], oacc, oacc[:, :], AF.Square)
        S.op(DVE, lambda e: e.tensor_reduce(out=ssq[:, :], in_=acc[:, :].rearrange("p (c d) -> p c d", d=128),
                                            axis=AX.X, op=ALU.add), reads=[acc], writes=[ssq])
        act(g, ssq, ssq[:, :], ssq, ssq[:, :], AF.Sqrt, bias=g.eps_c[:, 0:1], scale=1.0 / 128, extra_reads=[g.eps_c])
        recip(g, ssq, ssq[:, :], ssq, ssq[:, :])
        act(g, acc, acc[:, :], zt, zt[:, :], AF.Silu)
        for c in range(NC_):
            cs = slice(c * 128, (c + 1) * 128)
            tq = t128[c % 2]
            stt(g, DVE, tq, tq[:, :], oacc, oacc[:, cs], ssq[:, c:c + 1], gain, gain[:, :], ALU.mult, ALU.mult, extra_reads=[ssq])
            tt(g, POOL, oacc, oacc[:, cs], tq, tq[:, :], acc, acc[:, cs], ALU.mult)
        for c4 in range(0, NC_, 4):
            n4 = min(4, NC_ - c4)
            pbk = g.pb[(c4 // 4) % 2]
            for cc in range(n4):
                c = c4 + cc
                tr(g, pbk, pbk[:, cc * 128:(cc + 1) * 128], oacc, oacc[:, c * 128:(c + 1) * 128])
            evac(yT, yT[:, c4 * 128:(c4 + n4) * 128], pbk, pbk[:, 0:n4 * 128])
        dma(g, g.mixT[1024 + h * 128:1024 + (h + 1) * 128, :], yT[:, :], reads=[yT], queue=ACT)
        S.barrier()


def phase_pre_odd(g, rt, l, x_src, blocks):
    for (t0, w, is_ctx) in blocks:
        v = 1 if is_ctx else 0
        load_x(g, rt, x_src, t0, w)
        prenorm(g, rt, l, (0, 1), v, w)
        hv = rt.ht[:, :].rearrange("p (c t) -> p c t", c=NCH)[:, :, 0:w]
        dma(g, g.hT.rearrange("(c p) t -> p c t", p=128)[:, :, t0:t0 + w], hv, reads=rt.hc, queue=ACT)


def phase_fourier(g, l, with_ctx):
    S = g.S
    hf = S.sb("f_hf", [128, 4 * SEQ], BF16)
    Hc = S.sb("f_Hc", [128, 32 * 512], BF16)
    Hs = S.sb("f_Hs", [128, 32 * 512], BF16)
    cn = [S.sb("f_cn%d" % i, [128, 512], BF16) for i in range(2)]
    ost = [S.sb("f_ost%d" % i, [128, 4 * 512], BF16) for i in range(2)]
    for i in range(2):
        bf, bfv = wload(g, wview(g.dftn[i], 0, 2, 0, 256), 2, 256)
        cp(g, DVE, cn[i], cn[i][:, :], bf, bf[:, 0:512])
    oi = 0
    segs = [(0, SEQ, g.dftl)]
    if with_ctx:
        segs.append((SEQ, CTX, g.dftc))
    for fb in range(4):
        for (tok0, L, tab) in segs:
            nlc = L // 128
            hv = hf[:, 0:4 * L].rearrange("p (c t) -> p c t", c=4)
            dma(g, hv, g.hT[fb * 512:(fb + 1) * 512, tok0:tok0 + L].rearrange("(c p) t -> p c t", p=128), writes=[hf])
            for lc in range(nlc):
                pc = g.pb[(lc % 2) * 2]
                ps = g.pb[(lc % 2) * 2 + 1]
                for (pb_, tabi) in ((pc, 0), (ps, 1)):
                    for gi in range(2):
                        for kk in range(2):
                            mm(g, pb_, pb_[:, gi * 256:(gi + 1) * 256], hf, hv[:, gi * 2 + kk, lc * 128:(lc + 1) * 128],
                               cn[tabi], cn[tabi][:, kk * 256:(kk + 1) * 256], start=(kk == 0), stop=(kk == 1))
                cp(g, ACT, Hc, Hc[:, lc * 512:(lc + 1) * 512], pc, pc[:, :])
                cp(g, DVE, Hs, Hs[:, lc * 512:(lc + 1) * 512], ps, ps[:, :])
            kw = min(512, L)
            for kb in range(L // kw):
                lstep = 2048 // kw
                for lc0 in range(0, nlc, lstep):
                    nl = min(lstep, nlc - lc0)
                    slabs = []
                    for tabi in range(2):
                        bf, bfv = wload(g, wview(tab[tabi], lc0, nl, kb * kw, kw), nl, kw)
                        slabs.append((bf, bfv))
                    for li in range(nl):
                        lc = lc0 + li
                        for fc in range(4):
                            pbk = g.pb[4 + fc]
                            mm(g, pbk, pbk[:, 0:kw], Hc, Hc[:, lc * 512 + fc * 128: lc * 512 + (fc + 1) * 128],
                               slabs[0][0], slabs[0][1][:, li, :], start=(lc == 0), stop=False)
                            mm(g, pbk, pbk[:, 0:kw], Hs, Hs[:, lc * 512 + fc * 128: lc * 512 + (fc + 1) * 128],
                               slabs[1][0], slabs[1][1][:, li, :], start=False, stop=(lc == nlc - 1))
                o = ost[oi % 2]
                oi += 1
                ov = o[:, :].rearrange("p (c t) -> p c t", c=4)
                for fc in range(4):
                    cp(g, ACT if fc % 2 == 0 else DVE, o, ov[:, fc, 0:kw], g.pb[4 + fc], g.pb[4 + fc][:, 0:kw])
                t0 = tok0 + kb * kw
                dma(g, g.mixT[fb * 512:(fb + 1) * 512, t0:t0 + kw].rearrange("(c p) t -> p c t", p=128), ov[:, :, 0:kw],
                    reads=[o], queue=ACT)


OPTS = {"layers": list(range(DEPTH)), "skip": set()}


def full_forward(g):
    S = g.S
    nc = g.nc
    phase_mod(g)
    skip = OPTS["skip"]
    for l in OPTS["layers"]:
        even = (l % 2 == 0)
        ctx_needed = any(j % 2 == 0 for j in range(l + 1, DEPTH))
        uses_ctx = even or ctx_needed
        x_src = g.xin if l == 0 else g.xs
        blocks_pre = BLOCKS if uses_ctx else BLOCKS[:-1]
        blocks_post = BLOCKS if ctx_needed else BLOCKS[:-1]
        if "nblk" in OPTS:
            blocks_pre = blocks_pre[:OPTS["nblk"]]
            blocks_post = blocks_post[:OPTS["nblk"]]
        if even:
            with nc.reset_on_exit():
                rt = RowTiles(g, with_ffn=False)
                if "pre" not in skip:
                    phase_pre_even(g, rt, l, x_src, blocks_pre)
                S.barrier()
            with nc.reset_on_exit():
                if "attn" not in skip:
                    phase_attn(g, l, ctx_needed)
                S.barrier()
            with nc.reset_on_exit():
                if "gdn" not in skip:
                    phase_gdn(g, l)
                S.barrier()
            wmix = g.w_out[l // 2]
        else:
            with nc.reset_on_exit():
                rt = RowTiles(g, with_ffn=False)
                phase_pre_odd(g, rt, l, x_src, blocks_pre)
                S.barrier()
            with nc.reset_on_exit():
                if "fourier" not in skip:
                    phase_fourier(g, l, ctx_needed)
                S.barrier()
            wmix = g.w_four[l // 2]
        with nc.reset_on_exit():
            rt = RowTiles(g)
            if "post" not in skip:
                phase_post(g, rt, l, wmix, x_src, blocks_post, l == DEPTH - 1)
            S.barrier()


def host_inputs(inp, b):
    f32 = np.float32
    m = {}
    m["xin"] = np.ascontiguousarray(np.concatenate([inp["x"][b].T, inp["ctx"][b].T], axis=1))
    cc = np.stack([inp["c"][b].reshape(16, 128).T, inp["c_ctx"].reshape(16, 128).T], axis=2)
    m["cc"] = np.ascontiguousarray(cc.reshape(128, 32))
    m["w_ada"] = inp["w_ada"]
    m["b_ada"] = np.ascontiguousarray(inp["b_ada"].reshape(4, 96, 128).transpose(2, 0, 1).reshape(128, 4 * 96))
    gs = np.stack([inp["g_pre_mix"], inp["g_post_mix"], inp["g_pre_ffn"], inp["g_post_ffn"]], 0)
    m["gains"] = np.ascontiguousarray(gs.reshape(4, 4, 16, 128).transpose(3, 0, 1, 2).reshape(128, 256))
    m["w_in"] = inp["w_in"]
    m["w_conv"] = np.ascontiguousarray(inp["w_conv"].reshape(2, 5, 24, 128).transpose(3, 0, 2, 1).reshape(128, 240))
    m["sink"] = np.ascontiguousarray(inp["attn_sink"].reshape(1, 16))
    m["alog"] = np.ascontiguousarray(inp["gdn_a_log"].reshape(2, 16).T)
    m["dtb"] = np.ascontiguousarray(inp["gdn_dt_bias"].reshape(2, 16).T)
    m["gnorm"] = np.ascontiguousarray(inp["gdn_norm"].reshape(1, 256))
    m["w_out"] = inp["w_out_mix"]
    m["w_four"] = inp["w_fourier"]
    m["w_gate"] = inp["w_gate"]
    m["w_up"] = inp["w_up"]
    m["w_down"] = inp["w_down"]
    return m


_TABLES = {}


def const_tables():
    if _TABLES:
        return _TABLES
    f64 = np.float64
    quarter = 32
    inv_freq = 10000.0 ** (-np.arange(quarter, dtype=f64) / quarter)
    rows = np.repeat(np.arange(SEQ // 64, dtype=f64), 64)
    cols = np.tile(np.arange(64, dtype=f64), SEQ // 64)
    ang = np.concatenate([rows[:, None] * inv_freq, cols[:, None] * inv_freq], axis=-1)
    cosT = np.cos(ang).T
    sinT = np.sin(ang).T
    _TABLES["ropec"] = np.ascontiguousarray(np.concatenate([cosT, cosT], 0).astype(np.float32))
    _TABLES["ropes"] = np.ascontiguousarray(np.concatenate([-sinT, sinT], 0).astype(np.float32))
    ii = np.arange(128)
    ms = [(ii[:, None] // 16 == ii[None, :] // 16)]
    for b in (16, 32, 64):
        ms.append(((ii[:, None] // b) % 2 == 1) & (ii[None, :] // b == ii[:, None] // b - 1))
    ms += [m_.T for m_ in ms[1:4]]
    _TABLES["gmask"] = np.ascontiguousarray(np.concatenate([m_.astype(np.float32) for m_ in ms], axis=1))
    n = np.arange(256, dtype=f64)
    an = 2 * np.pi * np.outer(n, n) / 256
    _TABLES["dftn"] = np.stack([np.cos(an) / 1024.0, -np.sin(an) / 1024.0]).astype(np.float32)
    _TABLES["dftc"] = np.stack([4 * np.cos(an), 4 * np.sin(an)]).astype(np.float32)
    ll = np.arange(SEQ, dtype=np.int64)
    ph = (np.outer(ll, ll) % SEQ).astype(f64) * (2 * np.pi / SEQ)
    _TABLES["dftl"] = np.stack([np.cos(ph), np.sin(ph)]).astype(np.float32)
    return _TABLES


_PROG = {}


def kernel(**inputs):
    inp = {k: np.asarray(v) for k, v in inputs.items()}
    if "g" not in _PROG:
        _PROG["g"] = build_program()
    g = _PROG["g"]
    tabs = const_tables()
    in_maps = []
    for core in range(8):
        m = host_inputs(inp, core % 4)
        m.update(tabs)
        in_maps.append(m)
    res = run_bass_kernel_spmd(g.nc, in_maps, core_ids=list(range(8)))
    out = np.stack([np.ascontiguousarray(res.results[b]["outT"].T) for b in range(4)], axis=0)
    return out.astype(np.float32)
```

```python
import math
import numpy as np
import concourse.bass as bass
import concourse.mybir as mybir
from concourse.bass_utils import run_bass_kernel_spmd

F32 = mybir.dt.float32
BF16 = mybir.dt.bfloat16
AF = mybir.ActivationFunctionType
ALU = mybir.AluOpType
AX = mybir.AxisListType

PE, ACT, DVE, POOL, SP = "pe", "act", "dve", "pool", "sp"
COMPUTE = (PE, ACT, DVE, POOL)


class Res:
    __slots__ = ("name", "t", "last_w", "readers", "dsem", "dcount", "excl")

    def __init__(self, name, t):
        self.name = name
        self.t = t
        self.last_w = None
        self.readers = []
        self.dsem = None
        self.dcount = 0
        self.excl = False

    def __getitem__(self, key):
        return self.t[key]


class View:
    def __init__(self, base, ap):
        self.base = base
        self.ap = ap

    def __getitem__(self, key):
        return self.ap[key]


class Op:
    __slots__ = ("eng", "fn", "deps", "signal", "cnt", "is_dma", "dres", "dtarget", "ndma")

    def __init__(self, eng, fn):
        self.eng = eng
        self.fn = fn
        self.deps = []
        self.signal = False
        self.cnt = None
        self.is_dma = False
        self.dres = None
        self.dtarget = 0
        self.ndma = 0


SAME_ENGINE_SYNC = True


class Sched:
    def __init__(self, nc):
        self.nc = nc
        self.ops = []
        self.res = []
        self.esem = {}

    def sb(self, name, shape, dtype):
        self.uid = getattr(self, "uid", 0) + 1
        name = "%s_u%d" % (name, self.uid)
        t = self.nc.alloc_sbuf_tensor(name, list(shape), dtype)
        r = Res(name, t)
        self.res.append(r)
        return r

    def ps(self, name, shape, dtype=F32):
        t = self.nc.alloc_psum_tensor(name, list(shape), dtype)
        r = Res(name, t)
        r.excl = True
        self.res.append(r)
        return r

    def sub(self, name, ap):
        r = Res(name, ap)
        self.res.append(r)
        return r

    def _add_deps(self, op, reads, writes):
        reads = [getattr(r, "base", r) for r in reads]
        writes = [getattr(r, "base", r) for r in writes]
        writes = writes + [r for r in reads if r.excl and r not in writes]
        reads = [r for r in reads if not r.excl]
        deps = []
        for r in reads:
            if r.last_w is not None:
                deps.append((r.last_w, "raw", r))
        for r in writes:
            if r.last_w is not None:
                deps.append((r.last_w, "waw", r))
            for rd in r.readers:
                deps.append((rd, "war", r))
        seen = set()
        for d, kind, r in deps:
            if d is op or id(d) in seen:
                continue
            if d.eng == op.eng and not d.is_dma and not op.is_dma:
                if op.eng == PE:
                    continue
                if kind == "war" or r.excl or not SAME_ENGINE_SYNC:
                    continue
            seen.add(id(d))
            op.deps.append(d)
            d.signal = True
        for r in reads:
            r.readers.append(op)
        for r in writes:
            r.last_w = op
            r.readers = []

    def op(self, eng, fn, reads=(), writes=()):
        o = Op(eng, fn)
        self._add_deps(o, reads, writes)
        self.ops.append(o)
        return o

    def dma(self, fns, reads=(), writes=(), queue=SP, sem_res=None):
        o = Op(queue, fns)
        o.is_dma = True
        o.ndma = len(fns)
        if sem_res is None:
            sem_res = (list(writes) + list(reads))[0]
        sem_res = getattr(sem_res, "base", sem_res)
        o.dres = sem_res
        self._add_deps(o, reads, writes)
        sem_res.dcount += 16 * o.ndma
        o.dtarget = sem_res.dcount
        self.ops.append(o)
        return o

    def barrier(self):
        self.ops.append(("barrier",))

    def emit(self):
        nc = self.nc
        for e in COMPUTE:
            self.esem[e] = nc.alloc_semaphore("sem_" + e)
        nd = 0
        for r in self.res:
            if r.dcount > 0:
                r.dsem = nc.alloc_semaphore("dsem%d" % nd)
                nd += 1
        last_by_eng = {}
        dma_res_tot = {}
        flat = []
        for o in self.ops:
            if isinstance(o, tuple):
                for e, lo in last_by_eng.items():
                    lo.signal = True
                flat.append(("barrier", dict(last_by_eng), dict(dma_res_tot)))
                continue
            flat.append(o)
            if o.is_dma:
                dma_res_tot[id(o.dres)] = (o.dres, o.dtarget)
            else:
                last_by_eng[o.eng] = o
        cnt = {e: 0 for e in COMPUTE}
        for o in flat:
            if isinstance(o, tuple):
                continue
            if not o.is_dma and o.signal:
                cnt[o.eng] += 1
                o.cnt = cnt[o.eng]
        streams = {e: [] for e in (PE, ACT, DVE, POOL, SP)}
        waited = {e: {} for e in streams}

        def need(eng, sem, val, out):
            key = id(sem)
            if waited[eng].get(key, 0) >= val:
                return
            waited[eng][key] = val
            out.append(("wait", sem, val))

        for o in flat:
            if isinstance(o, tuple):
                _, lasts, dtot = o
                for eng in streams:
                    out = streams[eng]
                    for e, lo in lasts.items():
                        if e != eng:
                            need(eng, self.esem[e], lo.cnt, out)
                    for _, (r, tot) in dtot.items():
                        need(eng, r.dsem, tot, out)
                continue
            out = streams[o.eng]
            for d in o.deps:
                if d.is_dma:
                    need(o.eng, d.dres.dsem, d.dtarget, out)
                else:
                    need(o.eng, self.esem[d.eng], d.cnt, out)
            out.append(("op", o))
        self.n_instr = {e: len(s) for e, s in streams.items()}
        self.n_sems = nd + 4
        esem = self.esem

        def run_stream(eng_name):
            def body(eng):
                for item in streams[eng_name]:
                    if item[0] == "wait":
                        eng.wait_ge(item[1], item[2])
                    else:
                        o = item[1]
                        if o.is_dma:
                            for f in o.fn:
                                f(eng).then_inc(o.dres.dsem, 16)
                        else:
                            ins = o.fn(eng)
                            if o.signal:
                                ins.then_inc(esem[o.eng], 1)
            return body

        with nc.Block() as block:
            block.tensor(run_stream(PE))
            block.scalar(run_stream(ACT))
            block.vector(run_stream(DVE))
            block.gpsimd(run_stream(POOL))
            block.sync(run_stream(SP))


D = 2048
NCH = 16
SEQ = 4096
CTX = 256
T = SEQ + CTX
DEPTH = 4
HID = 5632
NHC = 44
IN_W = 5664
EPS = 1e-6
TB = 512
BLOCKS = [(i * TB, TB, False) for i in range(SEQ // TB)] + [(SEQ, CTX, True)]
OFF_Q, OFF_K, OFF_V, OFF_G, OFF_Z, OFF_A, OFF_B = 0, 1024, 1280, 1536, 4608, 5632, 5648


class G:
    pass


def build_program(body=None, dbg=None, mix_input=False):
    nc = bass.Bass("TRN2", target_bir_lowering=False)
    S = Sched(nc)
    g = G()
    g.nc, g.S = nc, S
    dbg = dbg or {}

    def din(name, shape, dt=F32):
        return nc.dram_tensor(name, list(shape), dt, kind="ExternalInput").ap()

    def dscr(name, shape, dt):
        kind = "ExternalOutput" if OPTS.get("dbg_scratch") else "Internal"
        return nc.dram_tensor(name, list(shape), dt, kind=kind).ap()

    g.xin = din("xin", [D, T])
    g.cc = din("cc", [128, NCH * 2])
    g.w_ada = din("w_ada", [DEPTH, D, 6 * D])
    g.b_ada = din("b_ada", [128, DEPTH * 96])
    g.gains = din("gains", [128, 4 * DEPTH * NCH])
    g.w_in = din("w_in", [2, D, IN_W])
    g.w_conv = din("w_conv", [128, 2 * 24 * 5])
    g.sink = din("sink", [1, 16])
    g.alog = din("alog", [16, 2])
    g.dtb = din("dtb", [16, 2])
    g.gnorm = din("gnorm", [1, 256])
    g.w_out = din("w_out", [2, D, D])
    g.w_four = din("w_four", [2, D, D])
    g.w_gate = din("w_gate", [DEPTH, D, HID])
    g.w_up = din("w_up", [DEPTH, D, HID])
    g.w_down = din("w_down", [DEPTH, HID, D])
    g.ropec = din("ropec", [128, SEQ])
    g.ropes = din("ropes", [128, SEQ])
    g.gmask = din("gmask", [128, 7 * 128])
    g.dftn = din("dftn", [2, 256, 256])
    g.dftl = din("dftl", [2, SEQ, SEQ])
    g.dftc = din("dftc", [2, CTX, CTX])
    g.out = nc.dram_tensor("outT", [D, SEQ], F32, kind="ExternalOutput").ap()
    g.xs = dscr("xs", [D, T], F32)
    g.mixT = din("mixT", [D, T], BF16) if mix_input else dscr("mixT", [D, T], BF16)
    g.hT = dscr("hT", [D, T], BF16)
    g.qT = dscr("qT", [1024, T], BF16)
    g.kT = dscr("kT", [256, T], BF16)
    g.vtm = dscr("vtm", [T, 256], BF16)
    g.gqkvT = dscr("gqkvT", [3072, T], BF16)
    g.ztm = dscr("ztm", [T, 1024], BF16)
    g.abT = dscr("abT", [32, T], F32)
    g.dbg_out = {}
    for name, (shape, dt) in dbg.items():
        g.dbg_out[name] = nc.dram_tensor("dbg_" + name, list(shape), dt, kind="ExternalOutput").ap()

    g.ones_bf = S.sb("ones_bf", [128, 128], BF16)
    g.ones_f = S.sb("ones_f", [128, 128], F32)
    g.ident = S.sb("ident", [128, 128], F32)
    g.eps_c = S.sb("eps_c", [128, 1], F32)
    g.mod = S.sb("mod", [128, DEPTH * 6 * NCH * 2], F32)
    g.gn = S.sb("gn", [128, 4 * DEPTH * NCH], F32)
    g.cols = S.sb("cols", [128, DEPTH * 6 * NCH * 2], F32)
    g.pb = [S.ps("pb%d" % i, [128, 512]) for i in range(8)]
    g.stg = [S.sb("stg%d" % i, [128, 2048], F32) for i in range(3)]
    g.wbf = [S.sb("wbf%d" % i, [128, 2048], BF16) for i in range(4)]
    g.stg_i = 0
    g.wbf_i = 0

    S.op(POOL, lambda e: e.memset(g.ones_bf[:, :], 1.0), writes=[g.ones_bf])
    S.op(POOL, lambda e: e.memset(g.ones_f[:, :], 1.0), writes=[g.ones_f])
    S.op(POOL, lambda e: e.memset(g.eps_c[:, :], EPS), writes=[g.eps_c])
    S.op(POOL, lambda e: e.memset(g.ident[:, :], 1.0), writes=[g.ident])
    S.op(POOL, lambda e: e.affine_select(out=g.ident[:, :], in_=g.ident[:, :], pattern=[[-1, 128]],
                                        compare_op=ALU.is_equal, fill=0.0, base=0, channel_multiplier=1),
         reads=[g.ident], writes=[g.ident])
    S.dma([lambda e: e.dma_start(out=g.gn[:, :], in_=g.gains[:, :])], writes=[g.gn])

    (body or full_forward)(g)
    S.barrier()
    S.emit()
    return g


CAST_PATTERN = (DVE, ACT, POOL, DVE, ACT)


def wload(g, src, kc, ncols):
    S = g.S
    st = g.stg[g.stg_i % len(g.stg)]
    g.stg_i += 1
    bf = g.wbf[g.wbf_i % len(g.wbf)]
    g.wbf_i += 1
    n = kc * ncols
    stv = st[:, 0:n].rearrange("p (a b) -> p a b", a=kc)
    bfv = bf[:, 0:n].rearrange("p (a b) -> p a b", a=kc)
    S.dma([lambda e: e.dma_start(out=stv, in_=src)], writes=[st])
    ce = CAST_PATTERN[g.wbf_i % len(CAST_PATTERN)]
    cp(g, ce, bf, bf[:, 0:n], st, st[:, 0:n])
    return bf, bfv


def wview(w2d, k0, kc, c0, ncols):
    return w2d.rearrange("(kc p) n -> p kc n", p=128)[:, k0:k0 + kc, c0:c0 + ncols]


def mm(g, pres, pap, lres, lap, rres, rap, start=True, stop=True):
    return g.S.op(PE, lambda e: e.matmul(pap, lhsT=lap, rhs=rap, start=start, stop=stop),
                  reads=[lres, rres], writes=[pres])


def tr(g, pres, pap, ires, iap):
    n = iap.shape[0]
    return g.S.op(PE, lambda e: e.transpose(out=pap, in_=iap, identity=g.ident[0:n, 0:n]),
                  reads=[ires, g.ident], writes=[pres])


def act(g, ores, oap, ires, iap, func, bias=None, scale=None, extra_reads=()):
    kw = {}
    if bias is not None:
        kw["bias"] = bias
    if scale is not None:
        kw["scale"] = scale
    rd = list(ires) if isinstance(ires, (list, tuple)) else [ires]
    return g.S.op(ACT, lambda e: e.activation(out=oap, in_=iap, func=func, **kw),
                  reads=rd + list(extra_reads), writes=[ores])


def tt(g, eng, ores, oap, r0, a0, r1, a1, op):
    return g.S.op(eng, lambda e: e.tensor_tensor(out=oap, in0=a0, in1=a1, op=op),
                  reads=[r0, r1], writes=[ores])


def stt(g, eng, ores, oap, r0, a0, sc, r1, a1, op0, op1, extra_reads=()):
    return g.S.op(eng, lambda e: e.scalar_tensor_tensor(out=oap, in0=a0, scalar=sc, in1=a1, op0=op0, op1=op1),
                  reads=[r0, r1] + list(extra_reads), writes=[ores])


def ts(g, eng, ores, oap, r0, a0, s1, s2, op0, op1=None, extra_reads=()):
    if op1 is None:
        return g.S.op(eng, lambda e: e.tensor_scalar(out=oap, in0=a0, scalar1=s1, scalar2=None, op0=op0),
                      reads=[r0] + list(extra_reads), writes=[ores])
    return g.S.op(eng, lambda e: e.tensor_scalar(out=oap, in0=a0, scalar1=s1, scalar2=s2, op0=op0, op1=op1),
                  reads=[r0] + list(extra_reads), writes=[ores])


def cp(g, eng, ores, oap, ires, iap):
    if eng == ACT:
        return act(g, ores, oap, ires, iap, AF.Copy)
    return g.S.op(eng, lambda e: e.tensor_copy(out=oap, in_=iap), reads=[ires], writes=[ores])


def recip(g, ores, oap, ires, iap):
    return g.S.op(DVE, lambda e: e.reciprocal(out=oap, in_=iap), reads=[ires], writes=[ores])


def dma(g, oap, iap, reads=(), writes=(), queue=SP):
    return g.S.dma([lambda e: e.dma_start(out=oap, in_=iap)], reads=list(reads), writes=list(writes), queue=queue)


def col(g, l, kind, ch, v):
    i = ((l * 6 + kind) * NCH + ch) * 2 + v
    return g.cols[:, i:i + 1]


def phase_mod(g):
    S = g.S
    with g.nc.reset_on_exit():
        sc = S.sb("sc", [128, NCH * 2], F32)
        sg = S.sb("sg", [128, NCH * 2], F32)
        bada = S.sb("bada", [128, DEPTH * 96], F32)
        dma(g, sc[:, :], g.cc[:, :], writes=[sc])
        dma(g, bada[:, :], g.b_ada[:, :], writes=[bada])
        act(g, sg, sg[:, :], sc, sc[:, :], AF.Sigmoid)
        tt(g, DVE, sc, sc[:, :], sc, sc[:, :], sg, sg[:, :], ALU.mult)
        for l in range(DEPTH):
            pb = g.pb[l % 2]
            for nj in range(96):
                st = g.stg[g.stg_i % len(g.stg)]
                g.stg_i += 1
                stv = st[:, :].rearrange("p (a b) -> p a b", a=16)
                dma(g, stv, wview(g.w_ada[l], 0, 16, nj * 128, 128), writes=[st])
                for k in range(16):
                    mm(g, pb, pb[:, nj * 2:nj * 2 + 2], st, stv[:, k, :], sc, sc[:, 2 * k:2 * k + 2],
                       start=(k == 0), stop=(k == 15))
            base = l * 6 * NCH * 2
            mv = g.mod[:, base:base + 192].rearrange("p (n v) -> p n v", v=2)
            pv = pb[:, 0:192].rearrange("p (n v) -> p n v", v=2)
            for v in range(2):
                tt(g, DVE, g.mod, mv[:, :, v], pb, pv[:, :, v], bada, bada[:, l * 96:(l + 1) * 96], ALU.add)
            def mview(j, v):
                return g.mod[:, base + j * 32: base + (j + 1) * 32].rearrange("p (c v) -> p c v", v=2)[:, :, v]

            def cview(kind, v):
                return g.cols[:, base + kind * 32: base + (kind + 1) * 32].rearrange("p (c v) -> p c v", v=2)[:, :, v]

            def gview(which):
                o = (which * DEPTH + l) * NCH
                return g.gn[:, o:o + NCH]

            for v in range(2):
                stt(g, DVE, g.cols, cview(0, v), g.mod, mview(1, v), 1.0, g.gn, gview(0), ALU.add, ALU.mult)
                cp(g, DVE, g.cols, cview(1, v), g.mod, mview(0, v))
                tt(g, DVE, g.cols, cview(2, v), g.mod, mview(2, v), g.gn, gview(1), ALU.mult)
                stt(g, DVE, g.cols, cview(3, v), g.mod, mview(4, v), 1.0, g.gn, gview(2), ALU.add, ALU.mult)
                cp(g, DVE, g.cols, cview(4, v), g.mod, mview(3, v))
                tt(g, DVE, g.cols, cview(5, v), g.mod, mview(5, v), g.gn, gview(3), ALU.mult)
        S.barrier()


class RowTiles:
    def __init__(self, g, with_ffn=True):
        S = g.S
        self.xt = S.sb("xt", [128, NCH * TB], F32)
        self.xc = [S.sub("xt%d" % j, self.xt[:, j * TB:(j + 1) * TB]) for j in range(NCH)]
        self.yt = S.sb("yt", [128, NCH * TB], F32)
        self.yc = [S.sub("yt%d" % j, self.yt[:, j * TB:(j + 1) * TB]) for j in range(NCH)]
        self.ht = S.sb("ht", [128, NCH * TB], BF16)
        self.hc = [S.sub("ht%d" % j, self.ht[:, j * TB:(j + 1) * TB]) for j in range(NCH)]
        if with_ffn:
            self.at = S.sb("at", [128, NHC * TB], BF16)
            self.ac = [S.sub("at%d" % j, self.at[:, j * TB:(j + 1) * TB]) for j in range(NHC)]
            self.mc = [self.ac[j] for j in range(NCH)]
        self.sq = [S.sb("sq%d" % i, [128, TB], BF16) for i in range(2)]
        self.tmp = [S.sb("tmp%d" % i, [128, TB], F32) for i in range(2)]
        self.rstd = S.sb("rstd", [128, TB], F32)
        self.i = 0


def ssq_step(g, rt, src_res, src_ap, w, first, last):
    sq = rt.sq[rt.i % 2]
    rt.i += 1
    act(g, sq, sq[:, 0:w], src_res, src_ap, AF.Square)
    mm(g, g.pb[2], g.pb[2][:, 0:w], g.ones_bf, g.ones_bf[:, :], sq, sq[:, 0:w], start=first, stop=last)


def rstd_finish(g, rt, w, nfeat):
    act(g, rt.rstd, rt.rstd[:, 0:w], g.pb[2], g.pb[2][:, 0:w], AF.Sqrt, bias=g.eps_c[:, 0:1], scale=1.0 / nfeat,
        extra_reads=[g.eps_c])
    recip(g, rt.rstd, rt.rstd[:, 0:w], rt.rstd, rt.rstd[:, 0:w])


def load_x(g, rt, src, t0, w):
    xv = rt.xt[:, :].rearrange("p (c t) -> p c t", c=NCH)[:, :, 0:w]
    dma(g, xv, src.rearrange("(c p) t -> p c t", p=128)[:, :, t0:t0 + w], writes=rt.xc)


def prenorm(g, rt, l, kinds, v, w):
    kg, ksh = kinds
    for j in range(NCH):
        ssq_step(g, rt, rt.xc[j], rt.xc[j][:, 0:w], w, j == 0, j == NCH - 1)
    rstd_finish(g, rt, w, D)
    for j in range(NCH):
        tmp = rt.tmp[j % 2]
        tt(g, DVE, tmp, tmp[:, 0:w], rt.xc[j], rt.xc[j][:, 0:w], rt.rstd, rt.rstd[:, 0:w], ALU.mult)
        act(g, rt.hc[j], rt.hc[j][:, 0:w], tmp, tmp[:, 0:w], AF.Identity,
            bias=col(g, l, ksh, j, v), scale=col(g, l, kg, j, v), extra_reads=[g.cols])


def resid_update(g, rt, l, kind_gg, v, w):
    rstd_finish(g, rt, w, D)
    for j in range(NCH):
        tmp = rt.tmp[j % 2]
        stt(g, DVE, tmp, tmp[:, 0:w], rt.yc[j], rt.yc[j][:, 0:w], col(g, l, kind_gg, j, v),
            rt.rstd, rt.rstd[:, 0:w], ALU.mult, ALU.mult, extra_reads=[g.cols])
        tt(g, DVE, rt.xc[j], rt.xc[j][:, 0:w], rt.xc[j], rt.xc[j][:, 0:w], tmp, tmp[:, 0:w], ALU.add)


def proj_to_yt(g, rt, w2d, kchunks, rhs_res, w):
    nsl = (kchunks + 15) // 16
    for j in range(NCH):
        pb = g.pb[j % 2]
        for s in range(nsl):
            k0 = s * 16
            kc = min(16, kchunks - k0)
            bf, bfv = wload(g, wview(w2d, k0, kc, j * 128, 128), kc, 128)
            for k in range(kc):
                kk = k0 + k
                mm(g, pb, pb[:, 0:w], bf, bfv[:, k, :], rhs_res[kk], rhs_res[kk][:, 0:w],
                   start=(kk == 0), stop=(kk == kchunks - 1))
        cp(g, ACT, rt.yc[j], rt.yc[j][:, 0:w], pb, pb[:, 0:w])
        ssq_step(g, rt, pb, pb[:, 0:w], w, j == 0, j == NCH - 1)


def phase_post(g, rt, l, wmix2d, x_src, blocks, final):
    for (t0, w, is_ctx) in blocks:
        v = 1 if is_ctx else 0
        load_x(g, rt, x_src, t0, w)
        mv = rt.at[:, 0:NCH * TB].rearrange("p (c t) -> p c t", c=NCH)[:, :, 0:w]
        dma(g, mv, g.mixT.rearrange("(c p) t -> p c t", p=128)[:, :, t0:t0 + w], writes=rt.mc)
        proj_to_yt(g, rt, wmix2d, NCH, rt.mc, w)
        resid_update(g, rt, l, 2, v, w)
        prenorm(g, rt, l, (3, 4), v, w)
        for n in range(NHC):
            pg = g.pb[3 + (n % 2) * 2]
            pu = g.pb[4 + (n % 2) * 2]
            bg, bgv = wload(g, wview(g.w_gate[l], 0, 16, n * 128, 128), 16, 128)
            for k in range(16):
                mm(g, pg, pg[:, 0:w], bg, bgv[:, k, :], rt.hc[k], rt.hc[k][:, 0:w], start=(k == 0), stop=(k == 15))
            bu, buv = wload(g, wview(g.w_up[l], 0, 16, n * 128, 128), 16, 128)
            for k in range(16):
                mm(g, pu, pu[:, 0:w], bu, buv[:, k, :], rt.hc[k], rt.hc[k][:, 0:w], start=(k == 0), stop=(k == 15))
            tmp = rt.tmp[n % 2]
            act(g, tmp, tmp[:, 0:w], pg, pg[:, 0:w], AF.Silu)
            tt(g, DVE, rt.ac[n], rt.ac[n][:, 0:w], tmp, tmp[:, 0:w], pu, pu[:, 0:w], ALU.mult)
        proj_to_yt(g, rt, g.w_down[l], NHC, rt.ac, w)
        resid_update(g, rt, l, 5, v, w)
        xv = rt.xt[:, :].rearrange("p (c t) -> p c t", c=NCH)[:, :, 0:w]
        if final and not is_ctx:
            dst = g.out.rearrange("(c p) t -> p c t", p=128)[:, :, t0:t0 + w]
        else:
            dst = g.xs.rearrange("(c p) t -> p c t", p=128)[:, :, t0:t0 + w]
        dma(g, dst, xv, reads=rt.xc, queue=ACT)


def phase_pre_even(g, rt, l, x_src, blocks):
    S = g.S
    e_i = l // 2
    w2d = g.w_in[e_i]
    rope_c = S.sb("rope_c", [128, TB], F32)
    rope_s = S.sb("rope_s", [128, TB], F32)
    pswap = S.sb("pswap", [128, 128], BF16)
    qraw = [S.sb("qraw%d" % i, [128, TB], BF16) for i in range(2)]
    stage = [S.sb("stage%d" % i, [128, TB], BF16) for i in range(2)]
    stage32 = [S.sb("stage32_%d" % i, [128, TB], F32) for i in range(2)]
    t1 = [S.sb("rt1_%d" % i, [128, TB], F32) for i in range(2)]
    pf = S.sb("pswapf", [128, 128], F32)
    S.op(POOL, lambda e: e.memset(pf[:, :], 0.0), writes=[pf])
    cp(g, DVE, pf, pf[:, 0:64], g.ident, g.ident[:, 64:128])
    cp(g, DVE, pf, pf[:, 64:128], g.ident, g.ident[:, 0:64])
    cp(g, DVE, pswap, pswap[:, :], pf, pf[:, :])
    si = 0
    for (t0, w, is_ctx) in blocks:
        v = 1 if is_ctx else 0
        load_x(g, rt, x_src, t0, w)
        if not is_ctx:
            dma(g, rope_c[:, 0:w], g.ropec[:, t0:t0 + w], writes=[rope_c])
            dma(g, rope_s[:, 0:w], g.ropes[:, t0:t0 + w], writes=[rope_s])
        prenorm(g, rt, l, (0, 1), v, w)
        fm_chunks = [("q", i, OFF_Q + i * 128, 128) for i in range(8)] + \
                    [("k", i, OFF_K + i * 128, 128) for i in range(2)] + \
                    [("g", i, OFF_G + i * 128, 128) for i in range(24)] + [("ab", 0, OFF_A, 32)]
        parts = OPTS.get("pre_parts", "qkgat")
        fm_chunks = [f for f in fm_chunks if f[0][0] in parts]
        for ci, (kind, idx, c0, ncol) in enumerate(fm_chunks):
            pb = g.pb[ci % 2]
            bf, bfv = wload(g, wview(w2d, 0, 16, c0, ncol), 16, ncol)
            for k in range(16):
                mm(g, pb, pb[0:ncol, 0:w], bf, bfv[:, k, :], rt.hc[k], rt.hc[k][:, 0:w], start=(k == 0), stop=(k == 15))
            if kind in ("q", "k"):
                dst = (g.qT if kind == "q" else g.kT)[idx * 128:(idx + 1) * 128, t0:t0 + w]
                st = stage[si % 2]
                if is_ctx or OPTS.get("norope"):
                    cp(g, ACT, st, st[:, 0:w], pb, pb[:, 0:w])
                else:
                    a1 = t1[si % 2]
                    prot = g.pb[3 + si % 2]
                    for k in range(16):
                        mm(g, prot, prot[0:64, 0:w], bf, bfv[:, k, 64:128], rt.hc[k], rt.hc[k][:, 0:w], start=(k == 0), stop=(k == 15))
                    for k in range(16):
                        mm(g, prot, prot[64:128, 0:w], bf, bfv[:, k, 0:64], rt.hc[k], rt.hc[k][:, 0:w], start=(k == 0), stop=(k == 15))
                    tt(g, DVE, a1, a1[:, 0:w], pb, pb[:, 0:w], rope_c, rope_c[:, 0:w], ALU.mult)
                    s32 = stage32[si % 2]
                    tt(g, DVE, s32, s32[:, 0:w], prot, prot[:, 0:w], rope_s, rope_s[:, 0:w], ALU.mult)
                    tt(g, DVE, st, st[:, 0:w], a1, a1[:, 0:w], s32, s32[:, 0:w], ALU.add)
                dma(g, dst, st[:, 0:w], reads=[st], queue=ACT)
                si += 1
            elif kind == "g":
                st = stage[si % 2]
                cp(g, ACT, st, st[:, 0:w], pb, pb[:, 0:w])
                dma(g, g.gqkvT[idx * 128:(idx + 1) * 128, t0:t0 + w], st[:, 0:w], reads=[st], queue=ACT)
                si += 1
            else:
                s32 = stage32[si % 2]
                cp(g, ACT, s32, s32[0:32, 0:w], pb, pb[0:32, 0:w])
                dma(g, g.abT[:, t0:t0 + w], s32[0:32, 0:w], reads=[s32], queue=ACT)
                si += 1
        nsub = w // 128
        tm_slabs = [("v", OFF_V + i * 128, i) for i in range(2)] + [("z", OFF_Z + i * 128, i) for i in range(8)]
        if "t" not in parts:
            tm_slabs = []
        for (kind, c0, idx) in tm_slabs:
            bf, bfv = wload(g, wview(w2d, 0, 16, c0, 128), 16, 128)
            pb = g.pb[5 + si % 2]
            for s in range(nsub):
                for k in range(16):
                    mm(g, pb, pb[:, s * 128:(s + 1) * 128], rt.hc[k], rt.hc[k][:, s * 128:(s + 1) * 128], bf, bfv[:, k, :],
                       start=(k == 0), stop=(k == 15))
            st = stage[si % 2]
            cp(g, ACT, st, st[:, 0:nsub * 128], pb, pb[:, 0:nsub * 128])
            dt_ = g.vtm if kind == "v" else g.ztm
            dst = dt_[t0:t0 + w, idx * 128:(idx + 1) * 128].rearrange("(s p) c -> p s c", p=128)
            dma(g, dst, st[:, 0:nsub * 128].rearrange("p (s c) -> p s c", c=128), reads=[st], queue=ACT)
            si += 1


def phase_attn(g, l, with_ctx_q):
    S = g.S
    e_i = l // 2
    kt = S.sb("a_kt", [128, T], BF16)
    vt = S.sb("a_vt", [128, 34 * 128], BF16)
    qt = [S.sb("a_qt%d" % i, [128, 4 * TB], BF16) for i in range(2)]
    ost = [S.sb("a_ost%d" % i, [128, 4 * TB], BF16) for i in range(2)]
    eb = [S.sb("a_eb%d" % i, [128, 512], BF16) for i in range(3)]
    ef = [S.sb("a_ef%d" % i, [128, 512], F32) for i in range(2)]
    mprev = S.sb("a_mprev", [128, 512], F32)
    mnext = S.sb("a_mnext", [128, 512], F32)
    skb = S.sb("a_skb", [128, 16], F32)
    esk = S.sb("a_esk", [128, 512], F32)
    rden = [S.sb("a_rden%d" % i, [128, 512], F32) for i in range(2)]
    for mt, cm in ((mprev, 1), (mnext, -1)):
        S.op(POOL, lambda e, mt=mt: e.memset(mt[:, :], 1.0), writes=[mt])
        S.op(POOL, lambda e, mt=mt, cm=cm: e.affine_select(
            out=mt[:, :].rearrange("p (g q) -> p g q", g=4), in_=mt[:, :].rearrange("p (g q) -> p g q", g=4),
            pattern=[[0, 4], [-cm, 128]], compare_op=ALU.is_ge, fill=0.0, base=0, channel_multiplier=cm),
            reads=[mt], writes=[mt])
    dma(g, skb[:, :], g.sink[0:1, :].partition_broadcast(128), writes=[skb])
    act(g, skb, skb[:, :], skb, skb[:, :], AF.Exp)
    scale = 128.0 ** -0.5
    gi = 0
    ei = 0
    for hk in range(2):
        dma(g, kt[:, :], g.kT[hk * 128:(hk + 1) * 128, :], writes=[kt])
        vtv = vt[:, :].rearrange("p (c d) -> p c d", d=128)
        vsrc = g.vtm[:, hk * 128:(hk + 1) * 128].rearrange("(c p) d -> p c d", p=128)
        S.dma([(lambda e, o_=vtv[:, a:a + 6, :], i_=vsrc[:, a:a + 6, :]: e.dma_start(out=o_, in_=i_)) for a in range(0, 30, 6)]
              + [lambda e, o_=vtv[:, 30:34, :], i_=vsrc[:, 30:34, :]: e.dma_start(out=o_, in_=i_)], writes=[vt])
        for gq in range(4):
            h = e_i * 8 + hk * 4 + gq
            ts(g, DVE, esk, esk[:, gq * 128:(gq + 1) * 128], g.ones_f, g.ones_f[:, :], skb[:, h:h + 1], None, ALU.mult,
               extra_reads=[skb])
        groups = [(i * TB, 4, False) for i in range(SEQ // TB)]
        if with_ctx_q:
            groups.append((SEQ, 2, True))
        for (t0, nqb, is_ctx) in groups:
            W = nqb * 128
            q = qt[gi % 2]
            o = ost[gi % 2]
            gi += 1
            qv = q[:, :].rearrange("p (g t) -> p g t", g=4)
            ov = o[:, :].rearrange("p (g t) -> p g t", g=4)
            dma(g, qv[:, :, 0:W], g.qT[hk * 512:(hk + 1) * 512, t0:t0 + W].rearrange("(g p) t -> p g t", p=128), writes=[q])
            for qi in range(nqb):
                qb = t0 // 128 + qi
                if is_ctx:
                    keys = [(32, None), (33, None)]
                else:
                    keys = []
                    if qb > 0:
                        keys.append((qb - 1, mprev))
                    keys.append((qb, None))
                    if qb < 31:
                        keys.append((qb + 1, mnext))
                    keys += [(32, None), (33, None)]
                po = g.pb[3 + qi % 2]
                pd = g.pb[5 + qi % 2]
                for i, (kc, m) in enumerate(keys):
                    ps = g.pb[i % 3]
                    mm(g, ps, ps[:, :].rearrange("p (g q) -> p g q", g=4), kt, kt[:, kc * 128:(kc + 1) * 128],
                       q, qv[:, :, qi * 128:(qi + 1) * 128])
                    e_ = eb[ei % 3]
                    if m is not None:
                        f_ = ef[ei % 2]
                        act(g, f_, f_[:, :], ps, ps[:, :], AF.Exp, scale=scale)
                        tt(g, DVE, e_, e_[:, :], f_, f_[:, :], m, m[:, :], ALU.mult)
                    else:
                        act(g, e_, e_[:, :], ps, ps[:, :], AF.Exp, scale=scale)
                    ei += 1
                    mm(g, po, po[:, :], vt, vt[:, kc * 128:(kc + 1) * 128], e_, e_[:, :], start=(i == 0), stop=(i == len(keys) - 1))
                    mm(g, pd, pd[:, :], g.ones_bf, g.ones_bf[:, :], e_, e_[:, :], start=(i == 0), stop=(i == len(keys) - 1))
                rd = rden[qi % 2]
                tt(g, DVE, rd, rd[:, :], pd, pd[:, :], esk, esk[:, :], ALU.add)
                recip(g, rd, rd[:, :], rd, rd[:, :])
                tt(g, DVE, o, ov[:, :, qi * 128:(qi + 1) * 128], po, po[:, :].rearrange("p (g q) -> p g q", g=4),
                   rd, rd[:, :].rearrange("p (g q) -> p g q", g=4), ALU.mult)
            dma(g, g.mixT[hk * 512:(hk + 1) * 512, t0:t0 + W].rearrange("(g p) t -> p g t", p=128), ov[:, :, 0:W],
                reads=[o], queue=ACT)


class GdnStop(Exception):
    pass


def phase_gdn(g, l):
    try:
        _phase_gdn(g, l)
    except GdnStop:
        g.S.barrier()


def _phase_gdn(g, l):
    S = g.S
    nc = g.nc
    e_i = l // 2
    NC_ = T // 128
    g_tm = S.sb("g_tm", [128, NC_ * 16 + 4], F32)
    b_tm = S.sb("b_tm", [128, NC_ * 16 + 4], F32)
    S.op(POOL, lambda e: e.memset(g_tm[:, :], 0.0), writes=[g_tm])
    pq = [[View(g.pb[b], g.pb[b][:, q * 128:(q + 1) * 128]) for q in range(4)] for b in range(8)]
    with nc.reset_on_exit():
        PW = T // 2
        aT = S.sb("gd_aT", [16, PW], F32)
        bT = S.sb("gd_bT", [16, PW], F32)
        al = S.sb("gd_al", [16, 2], F32)
        db = S.sb("gd_db", [16, 2], F32)
        dma(g, al[:, :], g.alog[:, :], writes=[al])
        dma(g, db[:, :], g.dtb[:, :], writes=[db])
        act(g, al, al[:, :], al, al[:, :], AF.Exp)
        ts(g, DVE, al, al[:, :], al, al[:, :], -1.0, None, ALU.mult)
        for pi in range(2):
            c0 = pi * PW
            dma(g, aT[:, :], g.abT[0:16, c0:c0 + PW], writes=[aT])
            dma(g, bT[:, :], g.abT[16:32, c0:c0 + PW], writes=[bT])
            act(g, aT, aT[:, :], aT, aT[:, :], AF.Exp, bias=db[:, e_i:e_i + 1], extra_reads=[db])
            act(g, aT, aT[:, :], aT, aT[:, :], AF.Ln, bias=g.ones_f[0:16, 0:1], extra_reads=[g.ones_f])
            ts(g, DVE, aT, aT[:, :], aT, aT[:, :], al[:, e_i:e_i + 1], None, ALU.mult, extra_reads=[al])
            act(g, bT, bT[:, :], bT, bT[:, :], AF.Exp, scale=-1.0)
            ts(g, DVE, bT, bT[:, :], bT, bT[:, :], 1.0, None, ALU.add)
            recip(g, bT, bT[:, :], bT, bT[:, :])
            ncp = PW // 128
            for (src, dst, bank) in ((aT, g_tm, 0), (bT, b_tm, 1)):
                pbk = g.pb[bank]
                for cc in range(ncp):
                    tr(g, pbk, pbk[:, cc * 16:(cc + 1) * 16], src, src[:, cc * 128:(cc + 1) * 128])
                cch = c0 // 128
                cp(g, ACT, dst, dst[:, cch * 16:(cch + ncp) * 16], pbk, pbk[:, 0:ncp * 16])
        S.barrier()
    if OPTS.get("gdn_stop", 9) <= 1:
        return
    wcv = S.sb("gd_wcv", [128, 2 * 24 * 5], F32)
    dma(g, wcv[:, :], g.w_conv[:, :], writes=[wcv])
    gain = S.sb("gd_gain", [128, 128], F32)
    dma(g, gain[:, :], g.gnorm[0:1, e_i * 128:(e_i + 1) * 128].partition_broadcast(128), writes=[gain])
    Lm = [S.sb("gd_L%d" % d, [128, 128], F32) for d in range(2)]
    Sm = [S.sb("gd_S%d" % d, [128, 128], F32) for d in range(2)]
    for (mt, cm, base) in ((Lm[0], -1, 0), (Lm[1], 1, 0), (Sm[0], 1, -1), (Sm[1], -1, -1)):
        S.op(POOL, lambda e, mt=mt: e.memset(mt[:, :], 1.0), writes=[mt])
        S.op(POOL, lambda e, mt=mt, cm=cm, base=base: e.affine_select(
            out=mt[:, :], in_=mt[:, :], pattern=[[-cm, 128]], compare_op=ALU.is_ge, fill=0.0, base=base,
            channel_multiplier=cm), reads=[mt], writes=[mt])
    gmk = S.sb("gd_gmk", [128, 7 * 128], F32)
    dma(g, gmk[:, :], g.gmask[:, :], writes=[gmk])
    raw = [S.sb("gd_raw%d" % i, [128, T], BF16) for i in range(2)]
    acc = S.sb("gd_acc", [128, T], F32)
    qTn = S.sb("gd_qTn", [128, T], F32)
    kTn = S.sb("gd_kTn", [128, T], F32)
    ktm = S.sb("gd_ktm", [128, T], F32)
    vtm = S.sb("gd_vtm", [128, T], BF16)
    oacc = S.sb("gd_oacc", [128, T], F32)
    sqb = [S.sb("gd_sq%d" % i, [128, 512], BF16) for i in range(2)]
    rn = S.sb("gd_rn", [128, 512], F32)
    ssq = S.sb("gd_ssq", [128, NC_], F32)
    t128 = [S.sb("gd_t%d" % i, [128, 128], F32) for i in range(2)]

    class DirT:
        pass
    DT_ = []
    for d in range(2):
        t_ = DirT()
        for nm in ("G1", "tpos", "tneg", "Dm", "DTt", "egcb", "A0", "A1", "At0", "At1", "R0", "R1", "AT", "vb", "kbg",
                   "kdec", "qd", "wT", "usb", "vnew", "St", "Ab", "Eb", "Tb", "Xp"):
            setattr(t_, nm, S.sb("gd_%s%d" % (nm, d), [128, 128], F32))
        cols_t = S.sb("gd_cols%d" % d, [128, 8], F32)
        t_.c = [S.sub("gd_c%d_%d" % (d, i), cols_t[:, i:i + 1]) for i in range(8)]
        t_.pi = 0
        DT_.append(t_)

    def pslot(d):
        t_ = DT_[d]
        i = t_.pi % 4
        t_.pi += 1
        return pq[4 * d + i][0]

    rri = [0]

    def evac(ores, oap, ires, iap):
        rri[0] += 1
        cp(g, ACT if rri[0] % 2 else DVE, ores, oap, ires, iap)

    def chk(k):
        if OPTS.get("gdn_chk", 99) <= k:
            raise GdnStop()

    wbase = e_i * 24 * 5
    for h in range(OPTS.get("gdn_heads", 8)):
        for part in range(3):
            ch = part * 8 + h
            rw = raw[part % 2]
            dma(g, rw[:, :], g.gqkvT[ch * 128:(ch + 1) * 128, :], writes=[rw])

            def wc(j):
                i = wbase + ch * 5 + j
                return wcv[:, i:i + 1]
            act(g, acc, acc[:, :], rw, rw[:, :], AF.Copy, scale=wc(2), extra_reads=[wcv])
            ti = 0
            for j in (0, 1, 3, 4):
                s_ = j - 2
                for (b0, ln) in ((0, SEQ), (SEQ, CTX)):
                    lo = b0 + max(0, -s_)
                    hi = b0 + min(ln, ln - s_)
                    stt(g, DVE, acc, acc[:, lo:hi], rw, rw[:, lo + s_:hi + s_], wc(j),
                        acc, acc[:, lo:hi], ALU.mult, ALU.add, extra_reads=[wcv])
                    ti += 1
            act(g, acc, acc[:, :], acc, acc[:, :], AF.Silu)
            if part < 2:
                dst = qTn if part == 0 else kTn
                sc_ = (128.0 ** -0.5) if part == 0 else 1.0
                for bi in range((T + 511) // 512):
                    c0 = bi * 512
                    w = min(512, T - c0)
                    sq = sqb[bi % 2]
                    pbk = g.pb[bi % 2]
                    act(g, sq, sq[:, 0:w], acc, acc[:, c0:c0 + w], AF.Square)
                    mm(g, pbk, pbk[:, 0:w], g.ones_bf, g.ones_bf[:, :], sq, sq[:, 0:w])
                    act(g, rn, rn[:, 0:w], pbk, pbk[:, 0:w], AF.Sqrt, bias=g.eps_c[:, 0:1], scale=1.0, extra_reads=[g.eps_c])
                    recip(g, rn, rn[:, 0:w], rn, rn[:, 0:w])
                    stt(g, DVE, dst, dst[:, c0:c0 + w], acc, acc[:, c0:c0 + w], sc_, rn, rn[:, 0:w], ALU.mult, ALU.mult)
            if part >= 1:
                src = kTn if part == 1 else acc
                dstm = ktm if part == 1 else vtm
                for c4 in range(0, NC_, 4):
                    n4 = min(4, NC_ - c4)
                    pbk = g.pb[2 + (c4 // 4) % 2]
                    for cc in range(n4):
                        c = c4 + cc
                        tr(g, pbk, pbk[:, cc * 128:(cc + 1) * 128], src, src[:, c * 128:(c + 1) * 128])
                    evac(dstm, dstm[:, c4 * 128:(c4 + n4) * 128], pbk, pbk[:, 0:n4 * 128])
        if OPTS.get("gdn_stop", 9) <= 2:
            return
        S.op(POOL, lambda e: e.memset(oacc[:, :], 0.0), writes=[oacc])
        for d in range(2):
            S.op(POOL, lambda e, d=d: e.memset(DT_[d].St[:, :], 0.0), writes=[DT_[d].St])
        S.barrier()
        order = [[32, 33] + list(range(32)), [33, 32] + list(range(31, -1, -1))]
        for step in range(OPTS.get("gdn_steps", NC_)):
            for d in range(2):
                c = order[d][step]
                t_ = DT_[d]
                colid = c * 16 + d * 8 + h
                gcol = g_tm[:, colid:colid + 1]
                bcol = b_tm[:, colid:colid + 1]
                last = 127 if d == 0 else 0
                cs = slice(c * 128, (c + 1) * 128)
                gccol, egcol, glc, kdsc, nbeta, bg = t_.c[0], t_.c[1], t_.c[2], t_.c[3], t_.c[4], t_.c[5]
                ts(g, DVE, t_.G1, t_.G1[:, :], g.ones_f, g.ones_f[:, :], gcol, None, ALU.mult, extra_reads=[g_tm])
                pgcb = pslot(d)
                mm(g, pgcb, pgcb[:, :], t_.G1, t_.G1[:, :], Lm[d], Lm[d][:, :])
                pgcc = pslot(d)
                mm(g, pgcc, pgcc[:, 0:2], Lm[d], Lm[d][:, :], g_tm, g_tm[:, colid:colid + 2])
                cp(g, ACT, gccol, gccol[:, :], pgcc, pgcc[:, 0:1])
                chk(1)
                ts(g, DVE, t_.tpos, t_.tpos[:, :], pgcb, pgcb[:, :], gccol[:, :], 0.0, ALU.subtract, ALU.max, extra_reads=[gccol])
                ts(g, DVE, t_.tneg, t_.tneg[:, :], pgcb, pgcb[:, :], gccol[:, :], 0.0, ALU.subtract, ALU.min, extra_reads=[gccol])
                act(g, t_.Dm, t_.Dm[:, :], t_.tpos, t_.tpos[:, :], AF.Exp, scale=-1.0)
                act(g, t_.DTt, t_.DTt[:, :], t_.tneg, t_.tneg[:, :], AF.Exp)
                act(g, t_.egcb, t_.egcb[:, :], pgcb, pgcb[:, :], AF.Exp)
                act(g, egcol, egcol[:, :], gccol, gccol[:, :], AF.Exp)
                cp(g, ACT, glc, glc[:, :], pgcb, pgcb[:, last:last + 1])
                act(g, kdsc, kdsc[:, :], gccol, gccol[:, :], AF.Exp, bias=glc[:, :], scale=-1.0, extra_reads=[glc])
                chk(2)
                tt(g, POOL, t_.Dm, t_.Dm[:, :], t_.Dm, t_.Dm[:, :], Sm[d], Sm[d][:, :], ALU.mult)
                tt(g, POOL, t_.DTt, t_.DTt[:, :], t_.DTt, t_.DTt[:, :], Lm[d], Lm[d][:, :], ALU.mult)
                ts(g, DVE, nbeta, nbeta[:, :], b_tm, bcol, -1.0, None, ALU.mult)
                tt(g, DVE, bg, bg[:, :], b_tm, bcol, egcol, egcol[:, :], ALU.mult)
                chk(3)
                pG = pslot(d)
                cp(g, DVE, t_.G1, t_.G1[:, :], kTn, kTn[:, cs])
                mm(g, pG, pG[:, :], kTn, kTn[:, cs], t_.G1, t_.G1[:, :])
                chk(3.2)
                pAT = pslot(d)
                mm(g, pAT, pAT[:, :], kTn, kTn[:, cs], qTn, qTn[:, cs])
                chk(3.4)
                tt(g, DVE, t_.A0, t_.A0[:, :], pG, pG[:, :], t_.Dm, t_.Dm[:, :], ALU.mult)
                ts(g, DVE, t_.A0, t_.A0[:, :], t_.A0, t_.A0[:, :], nbeta[:, :], None, ALU.mult, extra_reads=[nbeta])
                chk(3.6)
                tt(g, DVE, t_.AT, t_.AT[:, :], pAT, pAT[:, :], t_.DTt, t_.DTt[:, :], ALU.mult)
                chk(4)
                pT = pslot(d)
                tr(g, pT, pT[:, :], t_.A0, t_.A0[:, :])
                chk(4.2)
                bd = gmk[:, 0:128]
                tt(g, POOL, t_.Ab, t_.Ab[:, :], t_.A0, t_.A0[:, :], gmk, bd, ALU.mult)
                tt(g, DVE, t_.At0, t_.At0[:, :], pT, pT[:, :], gmk, bd, ALU.mult)
                chk(4.4)
                A = [t_.Ab, t_.A1]
                At = [t_.At0, t_.At1]
                R = [t_.R0, t_.R1]
                tt(g, DVE, R[0], R[0][:, :], At[0], At[0][:, :], g.ident, g.ident[:, :], ALU.add)
                chk(5)
                ri = 0
                for j in range(3):
                    a, b = j % 2, (j + 1) % 2
                    pA = pslot(d)
                    mm(g, pA, pA[:, :], At[a], At[a][:, :], A[a], A[a][:, :])
                    if j < 2:
                        pAt = pslot(d)
                        mm(g, pAt, pAt[:, :], A[a], A[a][:, :], At[a], At[a][:, :])
                    evac(A[b], A[b][:, :], pA, pA[:, :])
                    if j < 2:
                        evac(At[b], At[b][:, :], pAt, pAt[:, :])
                    pR = pslot(d)
                    mm(g, pR, pR[:, :], A[b], A[b][:, :], R[ri], R[ri][:, :])
                    tt(g, DVE, R[1 - ri], R[1 - ri][:, :], pR, pR[:, :], R[ri], R[ri][:, :], ALU.add)
                    ri = 1 - ri
                for lv in range(3):
                    off = gmk[:, (1 + 3 * d + lv) * 128:(2 + 3 * d + lv) * 128]
                    P = R[ri]
                    pTb = pslot(d)
                    tr(g, pTb, pTb[:, :], P, P[:, :])
                    tt(g, POOL, t_.Eb, t_.Eb[:, :], t_.A0, t_.A0[:, :], gmk, off, ALU.mult)
                    evac(t_.Tb, t_.Tb[:, :], pTb, pTb[:, :])
                    pX = pslot(d)
                    mm(g, pX, pX[:, :], t_.Eb, t_.Eb[:, :], P, P[:, :])
                    evac(t_.Xp, t_.Xp[:, :], pX, pX[:, :])
                    pY = pslot(d)
                    mm(g, pY, pY[:, :], t_.Tb, t_.Tb[:, :], t_.Xp, t_.Xp[:, :])
                    tt(g, DVE, R[1 - ri], R[1 - ri][:, :], pY, pY[:, :], P, P[:, :], ALU.add)
                    ri = 1 - ri
                Rf = R[ri]
                chk(6)
                ts(g, DVE, t_.vb, t_.vb[:, :], vtm, vtm[:, cs], bcol, None, ALU.mult, extra_reads=[b_tm])
                ts(g, DVE, t_.kbg, t_.kbg[:, :], ktm, ktm[:, cs], bg[:, :], None, ALU.mult, extra_reads=[bg])
                ts(g, DVE, t_.kdec, t_.kdec[:, :], ktm, ktm[:, cs], kdsc[:, :], None, ALU.mult, extra_reads=[kdsc])
                tt(g, DVE, t_.qd, t_.qd[:, :], qTn, qTn[:, cs], t_.egcb, t_.egcb[:, :], ALU.mult)
                pu = pslot(d)
                mm(g, pu, pu[:, :], Rf, Rf[:, :], t_.vb, t_.vb[:, :])
                pw = pslot(d)
                mm(g, pw, pw[:, :], t_.kbg, t_.kbg[:, :], Rf, Rf[:, :])
                evac(t_.usb, t_.usb[:, :], pu, pu[:, :])
                evac(t_.wT, t_.wT[:, :], pw, pw[:, :])
                chk(7)
                pws = pslot(d)
                mm(g, pws, pws[:, :], t_.wT, t_.wT[:, :], t_.St, t_.St[:, :])
                tt(g, DVE, t_.vnew, t_.vnew[:, :], t_.usb, t_.usb[:, :], pws, pws[:, :], ALU.subtract)
                po = pslot(d)
                mm(g, po, po[:, :], t_.qd, t_.qd[:, :], t_.St, t_.St[:, :], start=True, stop=False)
                mm(g, po, po[:, :], t_.AT, t_.AT[:, :], t_.vnew, t_.vnew[:, :], start=False, stop=True)
                tt(g, DVE, oacc, oacc[:, cs], oacc, oacc[:, cs], po, po[:, :], ALU.add)
                pkv = pslot(d)
                mm(g, pkv, pkv[:, :], t_.kdec, t_.kdec[:, :], t_.vnew, t_.vnew[:, :])
                stt(g, DVE, t_.St, t_.St[:, :], t_.St, t_.St[:, :], t_.egcb[:, last:last + 1], pkv, pkv[:, :],
                    ALU.mult, ALU.add, extra_reads=[t_.egcb])
        S.barrier()
        if OPTS.get("gdn_stop", 9) <= 3:
            return
        zt = raw[0]
        yT = raw[1]
        ztv = zt[:, :].rearrange("p (c d) -> p c d", d=128)
        zsrc = g.ztm[:, h * 128:(h + 1) * 128].rearrange("(c p) d -> p c d", p=128)
        S.dma([(lambda e, o_=ztv[:, a:a + 6, :], i_=zsrc[:, a:a + 6, :]: e.dma_start(out=o_, in_=i_)) for a in range(0, 30, 6)]
              + [lambda e, o_=ztv[:, 30:34, :], i_=zsrc[:, 30:34, :]: e.dma_start(out=o_, in_=i_)], writes=[zt])
        act(g, acc, acc[:, :], oacc, oacc[:, :], AF.Square)
        S.op(DVE, lambda e: e.tensor_reduce(out=ssq[:, :], in_=acc[:, :].rearrange("p (c d) -> p c d", d=128),
                                            axis=AX.X, op=ALU.add), reads=[acc], writes=[ssq])
        act(g, ssq, ssq[:, :], ssq, ssq[:, :], AF.Sqrt, bias=g.eps_c[:, 0:1], scale=1.0 / 128, extra_reads=[g.eps_c])
        recip(g, ssq, ssq[:, :], ssq, ssq[:, :])
        act(g, acc, acc[:, :], zt, zt[:, :], AF.Silu)
        for c in range(NC_):
            cs = slice(c * 128, (c + 1) * 128)
            tq = t128[c % 2]
            stt(g, DVE, tq, tq[:, :], oacc, oacc[:, cs], ssq[:, c:c + 1], gain, gain[:, :], ALU.mult, ALU.mult, extra_reads=[ssq])
            tt(g, POOL, oacc, oacc[:, cs], tq, tq[:, :], acc, acc[:, cs], ALU.mult)
        for c4 in range(0, NC_, 4):
            n4 = min(4, NC_ - c4)
            pbk = g.pb[(c4 // 4) % 2]
            for cc in range(n4):
                c = c4 + cc
                tr(g, pbk, pbk[:, cc * 128:(cc + 1) * 128], oacc, oacc[:, c * 128:(c + 1) * 128])
            evac(yT, yT[:, c4 * 128:(c4 + n4) * 128], pbk, pbk[:, 0:n4 * 128])
        dma(g, g.mixT[1024 + h * 128:1024 + (h + 1) * 128, :], yT[:, :], reads=[yT], queue=ACT)
        S.barrier()


def phase_pre_odd(g, rt, l, x_src, blocks):
    for (t0, w, is_ctx) in blocks:
        v = 1 if is_ctx else 0
        load_x(g, rt, x_src, t0, w)
        prenorm(g, rt, l, (0, 1), v, w)
        hv = rt.ht[:, :].rearrange("p (c t) -> p c t", c=NCH)[:, :, 0:w]
        dma(g, g.hT.rearrange("(c p) t -> p c t", p=128)[:, :, t0:t0 + w], hv, reads=rt.hc, queue=ACT)


def phase_fourier(g, l, with_ctx):
    S = g.S
    hf = S.sb("f_hf", [128, 4 * SEQ], BF16)
    Hc = S.sb("f_Hc", [128, 32 * 512], BF16)
    Hs = S.sb("f_Hs", [128, 32 * 512], BF16)
    cn = [S.sb("f_cn%d" % i, [128, 512], BF16) for i in range(2)]
    ost = [S.sb("f_ost%d" % i, [128, 4 * 512], BF16) for i in range(2)]
    for i in range(2):
        bf, bfv = wload(g, wview(g.dftn[i], 0, 2, 0, 256), 2, 256)
        cp(g, DVE, cn[i], cn[i][:, :], bf, bf[:, 0:512])
    oi = 0
    segs = [(0, SEQ, g.dftl)]
    if with_ctx:
        segs.append((SEQ, CTX, g.dftc))
    for fb in range(4):
        for (tok0, L, tab) in segs:
            nlc = L // 128
            hv = hf[:, 0:4 * L].rearrange("p (c t) -> p c t", c=4)
            dma(g, hv, g.hT[fb * 512:(fb + 1) * 512, tok0:tok0 + L].rearrange("(c p) t -> p c t", p=128), writes=[hf])
            for lc in range(nlc):
                pc = g.pb[(lc % 2) * 2]
                ps = g.pb[(lc % 2) * 2 + 1]
                for (pb_, tabi) in ((pc, 0), (ps, 1)):
                    for gi in range(2):
                        for kk in range(2):
                            mm(g, pb_, pb_[:, gi * 256:(gi + 1) * 256], hf, hv[:, gi * 2 + kk, lc * 128:(lc + 1) * 128],
                               cn[tabi], cn[tabi][:, kk * 256:(kk + 1) * 256], start=(kk == 0), stop=(kk == 1))
                cp(g, ACT, Hc, Hc[:, lc * 512:(lc + 1) * 512], pc, pc[:, :])
                cp(g, DVE, Hs, Hs[:, lc * 512:(lc + 1) * 512], ps, ps[:, :])
            kw = min(512, L)
            for kb in range(L // kw):
                lstep = 2048 // kw
                for lc0 in range(0, nlc, lstep):
                    nl = min(lstep, nlc - lc0)
                    slabs = []
                    for tabi in range(2):
                        bf, bfv = wload(g, wview(tab[tabi], lc0, nl, kb * kw, kw), nl, kw)
                        slabs.append((bf, bfv))
                    for li in range(nl):
                        lc = lc0 + li
                        for fc in range(4):
                            pbk = g.pb[4 + fc]
                            mm(g, pbk, pbk[:, 0:kw], Hc, Hc[:, lc * 512 + fc * 128: lc * 512 + (fc + 1) * 128],
                               slabs[0][0], slabs[0][1][:, li, :], start=(lc == 0), stop=False)
                            mm(g, pbk, pbk[:, 0:kw], Hs, Hs[:, lc * 512 + fc * 128: lc * 512 + (fc + 1) * 128],
                               slabs[1][0], slabs[1][1][:, li, :], start=False, stop=(lc == nlc - 1))
                o = ost[oi % 2]
                oi += 1
                ov = o[:, :].rearrange("p (c t) -> p c t", c=4)
                for fc in range(4):
                    cp(g, ACT if fc % 2 == 0 else DVE, o, ov[:, fc, 0:kw], g.pb[4 + fc], g.pb[4 + fc][:, 0:kw])
                t0 = tok0 + kb * kw
                dma(g, g.mixT[fb * 512:(fb + 1) * 512, t0:t0 + kw].rearrange("(c p) t -> p c t", p=128), ov[:, :, 0:kw],
                    reads=[o], queue=ACT)


OPTS = {"layers": list(range(DEPTH)), "skip": set()}


def full_forward(g):
    S = g.S
    nc = g.nc
    phase_mod(g)
    skip = OPTS["skip"]
    for l in OPTS["layers"]:
        even = (l % 2 == 0)
        ctx_needed = any(j % 2 == 0 for j in range(l + 1, DEPTH))
        uses_ctx = even or ctx_needed
        x_src = g.xin if l == 0 else g.xs
        blocks_pre = BLOCKS if uses_ctx else BLOCKS[:-1]
        blocks_post = BLOCKS if ctx_needed else BLOCKS[:-1]
        if "nblk" in OPTS:
            blocks_pre = blocks_pre[:OPTS["nblk"]]
            blocks_post = blocks_post[:OPTS["nblk"]]
        if even:
            with nc.reset_on_exit():
                rt = RowTiles(g, with_ffn=False)
                if "pre" not in skip:
                    phase_pre_even(g, rt, l, x_src, blocks_pre)
                S.barrier()
            with nc.reset_on_exit():
                if "attn" not in skip:
                    phase_attn(g, l, ctx_needed)
                S.barrier()
            with nc.reset_on_exit():
                if "gdn" not in skip:
                    phase_gdn(g, l)
                S.barrier()
            wmix = g.w_out[l // 2]
        else:
            with nc.reset_on_exit():
                rt = RowTiles(g, with_ffn=False)
                phase_pre_odd(g, rt, l, x_src, blocks_pre)
                S.barrier()
            with nc.reset_on_exit():
                if "fourier" not in skip:
                    phase_fourier(g, l, ctx_needed)
                S.barrier()
            wmix = g.w_four[l // 2]
        with nc.reset_on_exit():
            rt = RowTiles(g)
            if "post" not in skip:
                phase_post(g, rt, l, wmix, x_src, blocks_post, l == DEPTH - 1)
            S.barrier()


def host_inputs(inp, b):
    f32 = np.float32
    m = {}
    m["xin"] = np.ascontiguousarray(np.concatenate([inp["x"][b].T, inp["ctx"][b].T], axis=1))
    cc = np.stack([inp["c"][b].reshape(16, 128).T, inp["c_ctx"].reshape(16, 128).T], axis=2)
    m["cc"] = np.ascontiguousarray(cc.reshape(128, 32))
    m["w_ada"] = inp["w_ada"]
    m["b_ada"] = np.ascontiguousarray(inp["b_ada"].reshape(4, 96, 128).transpose(2, 0, 1).reshape(128, 4 * 96))
    gs = np.stack([inp["g_pre_mix"], inp["g_post_mix"], inp["g_pre_ffn"], inp["g_post_ffn"]], 0)
    m["gains"] = np.ascontiguousarray(gs.reshape(4, 4, 16, 128).transpose(3, 0, 1, 2).reshape(128, 256))
    m["w_in"] = inp["w_in"]
    m["w_conv"] = np.ascontiguousarray(inp["w_conv"].reshape(2, 5, 24, 128).transpose(3, 0, 2, 1).reshape(128, 240))
    m["sink"] = np.ascontiguousarray(inp["attn_sink"].reshape(1, 16))
    m["alog"] = np.ascontiguousarray(inp["gdn_a_log"].reshape(2, 16).T)
    m["dtb"] = np.ascontiguousarray(inp["gdn_dt_bias"].reshape(2, 16).T)
    m["gnorm"] = np.ascontiguousarray(inp["gdn_norm"].reshape(1, 256))
    m["w_out"] = inp["w_out_mix"]
    m["w_four"] = inp["w_fourier"]
    m["w_gate"] = inp["w_gate"]
    m["w_up"] = inp["w_up"]
    m["w_down"] = inp["w_down"]
    return m


_TABLES = {}


def const_tables():
    if _TABLES:
        return _TABLES
    f64 = np.float64
    quarter = 32
    inv_freq = 10000.0 ** (-np.arange(quarter, dtype=f64) / quarter)
    rows = np.repeat(np.arange(SEQ // 64, dtype=f64), 64)
    cols = np.tile(np.arange(64, dtype=f64), SEQ // 64)
    ang = np.concatenate([rows[:, None] * inv_freq, cols[:, None] * inv_freq], axis=-1)
    cosT = np.cos(ang).T
    sinT = np.sin(ang).T
    _TABLES["ropec"] = np.ascontiguousarray(np.concatenate([cosT, cosT], 0).astype(np.float32))
    _TABLES["ropes"] = np.ascontiguousarray(np.concatenate([-sinT, sinT], 0).astype(np.float32))
    ii = np.arange(128)
    ms = [(ii[:, None] // 16 == ii[None, :] // 16)]
    for b in (16, 32, 64):
        ms.append(((ii[:, None] // b) % 2 == 1) & (ii[None, :] // b == ii[:, None] // b - 1))
    ms += [m_.T for m_ in ms[1:4]]
    _TABLES["gmask"] = np.ascontiguousarray(np.concatenate([m_.astype(np.float32) for m_ in ms], axis=1))
    n = np.arange(256, dtype=f64)
    an = 2 * np.pi * np.outer(n, n) / 256
    _TABLES["dftn"] = np.stack([np.cos(an) / 1024.0, -np.sin(an) / 1024.0]).astype(np.float32)
    _TABLES["dftc"] = np.stack([4 * np.cos(an), 4 * np.sin(an)]).astype(np.float32)
    ll = np.arange(SEQ, dtype=np.int64)
    ph = (np.outer(ll, ll) % SEQ).astype(f64) * (2 * np.pi / SEQ)
    _TABLES["dftl"] = np.stack([np.cos(ph), np.sin(ph)]).astype(np.float32)
    return _TABLES


_PROG = {}


def kernel(**inputs):
    inp = {k: np.asarray(v) for k, v in inputs.items()}
    if "g" not in _PROG:
        _PROG["g"] = build_program()
    g = _PROG["g"]
    tabs = const_tables()
    in_maps = []
    for core in range(8):
        m = host_inputs(inp, core % 4)
        m.update(tabs)
        in_maps.append(m)
    res = run_bass_kernel_spmd(g.nc, in_maps, core_ids=list(range(8)))
    out = np.stack([np.ascontiguousarray(res.results[b]["outT"].T) for b in range(4)], axis=0)
    return out.astype(np.float32)
```
